# Optimizing a Trainium2 kernel written in Bass

```python
import jax, jax.numpy as jnp
from jax import lax
import numpy as np

D_MODEL = 1024
BATCH = 8
SEQ = 4096
DEPTH = 1

N_HEADS = 8
HEAD_DIM = 128
ATTN_WIDTH = N_HEADS * HEAD_DIM
Q_BLOCK = 128
LRU_WIDTH = 1536
LRU_BLOCKS = 12
LRU_BLOCK = LRU_WIDTH // LRU_BLOCKS
CONV_WIDTH = 4
LRU_C = 8.0
FFN_HIDDEN = -(-8 * D_MODEL // (3 * 256)) * 256
N_MOD = 6
EPS = 1e-6
IN_WIDTHS = (ATTN_WIDTH, ATTN_WIDTH, ATTN_WIDTH, LRU_WIDTH, LRU_WIDTH, D_MODEL, D_MODEL)
IN_TOTAL = sum(IN_WIDTHS)

kernel_name = "hybrid_stickbreak_rglru_block"


def rms_norm(x, g):
    xf = x.astype(jnp.float32)
    y = xf * lax.rsqrt(jnp.mean(xf * xf, axis=-1, keepdims=True) + EPS)
    return (y * g.astype(jnp.float32)).astype(x.dtype)


def stick_breaking_attention(q, k, v):
    S, Dh = q.shape[2], q.shape[3]
    scale = Dh ** -0.5
    outs = []
    for blk in range(S // Q_BLOCK):
        q0, q1 = blk * Q_BLOCK, (blk + 1) * Q_BLOCK
        qb = q[:, :, q0:q1]
        kb = k[:, :, :q1]
        vb = v[:, :, :q1]
        z = jnp.einsum('bhtd,bhsd->bhts', qb, kb).astype(jnp.float32) * scale
        t_idx = jnp.arange(q0, q1)[:, None]
        s_idx = jnp.arange(q1)[None, :]
        causal = s_idx < t_idx
        log_beta = jax.nn.log_sigmoid(z)
        log_one_minus = jnp.where(causal, jax.nn.log_sigmoid(-z), 0.0)
        rc = lax.cumsum(log_one_minus, axis=3, reverse=True)
        suffix = jnp.pad(rc[..., 1:], ((0, 0), (0, 0), (0, 0), (0, 1)))
        w = jnp.where(causal, jnp.exp(log_beta + suffix), 0.0)
        outs.append(jnp.einsum('bhts,bhsd->bhtd', w, vb.astype(jnp.float32)))
    return jnp.concatenate(outs, axis=2).astype(q.dtype)


def causal_depthwise_conv(x, w, b):
    S = x.shape[1]
    xp = jnp.pad(x, ((0, 0), (CONV_WIDTH - 1, 0), (0, 0)))
    y = b
    for kk in range(CONV_WIDTH):
        y = y + xp[:, kk:kk + S] * w[kk]
    return y


def block_diag_linear(x, w, b):
    Bsz, S, W = x.shape
    xb = x.reshape(Bsz, S, LRU_BLOCKS, LRU_BLOCK)
    return jnp.einsum('bsni,nij->bsnj', xb, w).reshape(Bsz, S, W) + b


def rg_lru(x, w_rg, b_rg, w_ig, b_ig, lam):
    r = jax.nn.sigmoid(block_diag_linear(x, w_rg, b_rg).astype(jnp.float32))
    i = jax.nn.sigmoid(block_diag_linear(x, w_ig, b_ig).astype(jnp.float32))
    log_a = -LRU_C * r * jax.nn.softplus(-lam.astype(jnp.float32))
    a = jnp.exp(log_a)
    mult = jnp.sqrt(-jnp.expm1(2.0 * log_a))
    u = mult * (i * x.astype(jnp.float32))

    def combine(left, right):
        a1, b1 = left
        a2, b2 = right
        return a1 * a2, a2 * b1 + b2

    _, h = lax.associative_scan(combine, (a, u), axis=1)
    return h.astype(x.dtype)


def setup_inputs(seed: int = 0) -> dict:
    key = jax.random.key(seed)
    ks = jax.random.split(key, 24)
    nrm = lambda k, shape, s: jax.random.normal(k, shape, jnp.float32) * s
    L, D = DEPTH, D_MODEL
    u = jax.random.uniform(ks[13], (L, LRU_WIDTH), jnp.float32, 0.9, 0.999)
    a0 = u ** (1.0 / LRU_C)
    lru_lambda = jnp.log(a0) - jnp.log1p(-a0)
    return {
        "x": nrm(ks[0], (BATCH, SEQ, D), 1.0),
        "c": nrm(ks[1], (BATCH, D), 1.0),
        "w_ada": nrm(ks[2], (L, D, N_MOD * D), 0.5 * D ** -0.5),
        "b_ada": nrm(ks[3], (L, N_MOD * D), 0.01),
        "norm1_g": 1.0 + nrm(ks[4], (L, D), 0.02),
        "w_in": nrm(ks[5], (L, D, IN_TOTAL), D ** -0.5),
        "q_norm_g": 1.0 + nrm(ks[6], (L, HEAD_DIM), 0.02),
        "k_norm_g": 1.0 + nrm(ks[7], (L, HEAD_DIM), 0.02),
        "conv_w": nrm(ks[8], (L, CONV_WIDTH, LRU_WIDTH), CONV_WIDTH ** -0.5),
        "conv_b": nrm(ks[9], (L, LRU_WIDTH), 0.01),
        "w_rg": nrm(ks[10], (L, LRU_BLOCKS, LRU_BLOCK, LRU_BLOCK), LRU_BLOCK ** -0.5),
        "b_rg": nrm(ks[11], (L, LRU_WIDTH), 0.01),
        "w_ig": nrm(ks[12], (L, LRU_BLOCKS, LRU_BLOCK, LRU_BLOCK), LRU_BLOCK ** -0.5),
        "b_ig": nrm(ks[14], (L, LRU_WIDTH), 0.01),
        "lru_lambda": lru_lambda,
        "w_proj_attn": nrm(ks[15], (L, ATTN_WIDTH, D), ATTN_WIDTH ** -0.5),
        "w_proj_lru": nrm(ks[16], (L, LRU_WIDTH, D), LRU_WIDTH ** -0.5),
        "w_out": nrm(ks[17], (L, D, D), D ** -0.5),
        "norm2_g": 1.0 + nrm(ks[18], (L, D), 0.02),
        "w_ffn_in": nrm(ks[19], (L, D, 2 * FFN_HIDDEN), D ** -0.5),
        "w_ffn_out": nrm(ks[20], (L, FFN_HIDDEN, D), FFN_HIDDEN ** -0.5),
    }


def reference(x, c, w_ada, b_ada, norm1_g, w_in, q_norm_g, k_norm_g, conv_w, conv_b,
              w_rg, b_rg, w_ig, b_ig, lru_lambda, w_proj_attn, w_proj_lru, w_out,
              norm2_g, w_ffn_in, w_ffn_out):
    Bsz, S, D = x.shape
    split_at = [int(v) for v in np.cumsum(IN_WIDTHS)[:-1]]
    c_act = jax.nn.silu(c)
    for l in range(DEPTH):
        mod = jnp.einsum('bd,de->be', c_act, w_ada[l]) + b_ada[l]
        shift1, scale1, gate1, shift2, scale2, gate2 = [
            m[:, None, :] for m in jnp.split(mod, N_MOD, axis=-1)]

        h = rms_norm(x, norm1_g[l]) * (1.0 + scale1) + shift1
        proj = jnp.einsum('bsd,de->bse', h, w_in[l])
        q, k, v, xr, gr, ga, gb = jnp.split(proj, split_at, axis=-1)

        q = rms_norm(q.reshape(Bsz, S, N_HEADS, HEAD_DIM), q_norm_g[l])
        k = rms_norm(k.reshape(Bsz, S, N_HEADS, HEAD_DIM), k_norm_g[l])
        v = v.reshape(Bsz, S, N_HEADS, HEAD_DIM)
        o = stick_breaking_attention(q.transpose(0, 2, 1, 3), k.transpose(0, 2, 1, 3),
                                     v.transpose(0, 2, 1, 3))
        o = o.transpose(0, 2, 1, 3).reshape(Bsz, S, ATTN_WIDTH)
        p_attn = jnp.einsum('bsa,ad->bsd', o, w_proj_attn[l])

        xc = causal_depthwise_conv(xr, conv_w[l], conv_b[l])
        y_lru = rg_lru(xc, w_rg[l], b_rg[l], w_ig[l], b_ig[l], lru_lambda[l])
        y_lru = jax.nn.gelu(gr) * y_lru
        p_lru = jnp.einsum('bsw,wd->bsd', y_lru, w_proj_lru[l])

        merged = jax.nn.sigmoid(ga) * p_attn + jax.nn.sigmoid(gb) * p_lru
        mix_out = jnp.einsum('bsd,de->bse', merged, w_out[l])
        x = x + gate1 * mix_out

        h2 = rms_norm(x, norm2_g[l]) * (1.0 + scale2) + shift2
        gu = jnp.einsum('bsd,df->bsf', h2, w_ffn_in[l])
        g_ffn, u_ffn = jnp.split(gu, 2, axis=-1)
        ffn_out = jnp.einsum('bsf,fd->bsd', jax.nn.silu(g_ffn) * u_ffn, w_ffn_out[l])
        x = x + gate2 * ffn_out
    return x
```

```python
import contextlib
import numpy as np
import concourse.bass as bass
import concourse.mybir as mybir
from concourse.bass_utils import run_bass_kernel_spmd

F32 = mybir.dt.float32
BF16 = mybir.dt.bfloat16
AF = mybir.ActivationFunctionType
ALU = mybir.AluOpType

S = 4096
D = 1024
NTILE = 32
NCH = 8
LRU_N = 12
FFN_N = 22
EPS = 1e-6
ENGS = ("pe", "act", "dve", "pool", "sp")
NDSEM = 26
NSSEM = 12


class Prog:
    def __init__(self, nc, st):
        self.nc = nc
        self.esem = {e: st.enter_context(nc.semaphore("s_" + e)) for e in ENGS if e != "sp"}
        self.dpool = ([st.enter_context(nc.semaphore("d%d" % i)) for i in range(NDSEM)]
                      + [st.enter_context(nc.semaphore("w%d" % i)) for i in range(NSSEM)])
        self.dbase = [0] * (NDSEM + NSSEM)
        self.msbase = {e: 0 for e in ENGS}
        self.ops = []

    def op(self, eng, fn, reads=(), writes=(), dma_key=None):
        self.ops.append(dict(eng=eng, fn=fn, reads=tuple(reads), writes=tuple(writes),
                             dma_key=dma_key))

    def emit(self, name):
        nc = self.nc
        ops = self.ops
        self.ops = []
        eng_count = {e: 0 for e in ENGS}
        dma_count = {}
        dma_slot = {}
        n_hw, n_sw, key_q = [0], [0], {}
        last_writer = {}
        readers = {}
        clock = {e: {} for e in ENGS}
        for i, o in enumerate(ops):
            e = o["eng"]
            deps = set()
            for b in o["reads"]:
                if b in last_writer:
                    deps.add(last_writer[b])
            for b in o["writes"]:
                if b in last_writer:
                    deps.add(last_writer[b])
                for r in readers.get(b, ()):
                    deps.add(r)
            deps.discard(i)
            need = {}
            ck = clock[e]
            for d in deps:
                od = ops[d]
                dim, idx = od["dim"], od["idx"]
                if dim == "pe" and e == "pe" and o["dma_key"] is None:
                    continue
                if ck.get(dim, 0) >= idx:
                    continue
                if need.get(dim, (0, None))[0] < idx:
                    need[dim] = (idx, d)
            for dim, (idx, d) in need.items():
                for k, v in ops[d]["vc"].items():
                    if ck.get(k, 0) < v:
                        ck[k] = v
            o["waits"] = [(dim, idx, d) for dim, (idx, d) in need.items()]
            if o["dma_key"] is None:
                eng_count[e] += 1
                o["dim"], o["idx"] = e, eng_count[e]
            else:
                k = o["dma_key"]
                if k not in dma_slot:
                    if e == "pool":
                        dma_slot[k] = NDSEM + n_sw[0]
                        n_sw[0] += 1
                        assert n_sw[0] <= NSSEM, "too many software dma keys"
                    else:
                        dma_slot[k] = n_hw[0]
                        n_hw[0] += 1
                        assert n_hw[0] <= NDSEM, "too many dma keys"
                    key_q[k] = e
                assert key_q[k] == e, "dma key used from two queues"
                dma_count[k] = dma_count.get(k, 0) + 1
                o["dim"], o["idx"] = ("dma", k), dma_count[k]
            vc = dict(ck)
            vc[o["dim"]] = o["idx"]
            o["vc"] = vc
            for b in o["reads"]:
                readers.setdefault(b, []).append(i)
            for b in o["writes"]:
                last_writer[b] = i
                readers[b] = []
        awaited = set()
        for o in ops:
            for dim, idx, d in o["waits"]:
                awaited.add(d)
        mcount = dict(self.msbase)
        for i, o in enumerate(ops):
            if o["dma_key"] is None:
                if i in awaited:
                    mcount[o["eng"]] += 1
                    o["ms"] = mcount[o["eng"]]
                else:
                    o["ms"] = None
        esem, dpool, dbase = self.esem, self.dpool, self.dbase

        def run(ename, eng):
            for o in ops:
                if o["eng"] != ename:
                    continue
                for dim, idx, d in o["waits"]:
                    if isinstance(dim, tuple):
                        sl = dma_slot[dim[1]]
                        eng.wait_ge(dpool[sl], 16 * (dbase[sl] + idx))
                    else:
                        eng.wait_ge(esem[dim], ops[d]["ms"])
                ins = o["fn"](eng)
                if o["dma_key"] is not None:
                    ins.then_inc(dpool[dma_slot[o["dma_key"]]], 16)
                elif o["ms"] is not None:
                    ins.then_inc(esem[ename], 1)
            if ename == "sp":
                for k, sl in dma_slot.items():
                    eng.wait_ge(dpool[sl], 16 * (dbase[sl] + dma_count[k]))

        with nc.Block() as block:
            @block.tensor
            def _(eng):
                run("pe", eng)

            @block.scalar
            def _(eng):
                run("act", eng)

            @block.vector
            def _(eng):
                run("dve", eng)

            @block.gpsimd
            def _(eng):
                run("pool", eng)

            @block.sync
            def _(eng):
                run("sp", eng)

        for k, sl in dma_slot.items():
            dbase[sl] += dma_count[k]
        self.msbase = mcount


def build_program(debug=False):
    nc = bass.Bass("TRN2", target_bir_lowering=False)
    dt_in = lambda name, shape: nc.dram_tensor(name, shape, F32, kind="ExternalInput").ap()
    x = dt_in("x", [S, D])
    c_col = dt_in("c_col", [128, 8])
    w_ada = dt_in("w_ada", [D, 6 * D])
    b_ada = dt_in("b_ada", [1, 6 * D])
    n1g = dt_in("n1g", [1, D])
    n2g = dt_in("n2g", [1, D])
    w_in = dt_in("w_in", [D, 8192])
    qg_col = dt_in("qg_col", [128, 1])
    kg_col = dt_in("kg_col", [128, 1])
    cw_col = dt_in("cw_col", [128, 4, LRU_N])
    cb_col = dt_in("cb_col", [128, LRU_N])
    brg_col = dt_in("brg_col", [128, LRU_N])
    big_col = dt_in("big_col", [128, LRU_N])
    lam_col = dt_in("lam_col", [128, LRU_N])
    w_rg = dt_in("w_rg", [LRU_N, 128, 128])
    w_ig = dt_in("w_ig", [LRU_N, 128, 128])
    wpa_d = dt_in("wpa", [D, D])
    wpl_d = dt_in("wpl", [1536, D])
    wo_d = dt_in("wo", [D, D])
    wf1_d = dt_in("wf1", [D, 5632])
    wf2_d = dt_in("wf2", [2816, D])
    out = nc.dram_tensor("out", [S, D], F32, kind="ExternalOutput").ap()
    skind = "ExternalOutput" if debug else "Internal"
    oT_d = nc.dram_tensor("oT_d", [8, 128, S], BF16, kind=skind).ap()
    yT_d = nc.dram_tensor("yT_d", [LRU_N, 128, S], BF16, kind=skind).ap()
    x1_d = nc.dram_tensor("x1_d", [S, D], F32, kind=skind).ap()
    aT_d = nc.dram_tensor("aT_d", [FFN_N, 128, S], BF16, kind=skind).ap()
    mA_d = nc.dram_tensor("mA_d", [8, 128, S], F32, kind=skind).ap()
    gl_d = nc.dram_tensor("gl_d", [LRU_N, 128, S], F32, kind=skind).ap()
    mods_d = nc.dram_tensor("mods_d", [128, 4, D], F32, kind=skind).ap()
    mT_d = nc.dram_tensor("mT_d", [8, 128, S], BF16, kind=skind).ap()

    wview = lambda w, c0, cn: w[:, c0:c0 + cn].rearrange("(k p) c -> p k c", p=128)

    with contextlib.ExitStack() as gst:
        GT = lambda name, shape, dt: gst.enter_context(nc.sbuf_tensor(name, shape, dt))
        P = Prog(nc, gst)
        PP = [gst.enter_context(nc.psum_tensor("PP%d" % i, [128, 1024], F32)) for i in range(4)]
        pb = []
        for i in range(4):
            pb += [PP[i][:, 0:512], PP[i][:, 512:1024]]
        pbt = pb[7].bitcast(BF16).rearrange("p (k t) -> p k t", k=8)
        hT = GT("hT", [128, 8, S], BF16)
        ident = GT("ident", [128, 128], BF16)
        ones_b = GT("ones_b", [128, 128], BF16)
        tri = GT("tri", [128, 128], BF16)
        ub = GT("ub", [128, 128], BF16)
        maskb = GT("maskb", [128, 4, 512], BF16)

        def dma(q, out_, in_, key, reads=(), writes=()):
            P.op(q, lambda e: e.dma_start(out=out_, in_=in_), reads=reads, writes=writes, dma_key=key)

        def mm_group(bank, pairs, reads, wkey):
            n = len(pairs)
            for i, (l, r) in enumerate(pairs):
                P.op("pe", lambda e, l=l, r=r, i=i: e.matmul(bank, lhsT=l, rhs=r, start=(i == 0),
                                                            stop=(i == n - 1)),
                     reads=reads, writes=[wkey])

        def norm_front(i, xt_ap, xt_key, gs_ap, sh_ap, bufs):
            junk, ss, sq, rstd, tmp, hb = bufs
            j = i % 2
            P.op("act", lambda e: e.activation(out=junk[:], in_=xt_ap, func=AF.Square, accum_out=ss[j][:]),
                 reads=[xt_key], writes=["junk", ("ss", j)])
            P.op("act", lambda e: e.activation(out=sq[j][:], in_=ss[j][:], func=AF.Sqrt, scale=1.0 / D, bias=EPS),
                 reads=[("ss", j)], writes=[("sq", j)])
            P.op("dve", lambda e: e.reciprocal(out=rstd[j][:], in_=sq[j][:]), reads=[("sq", j)], writes=[("rstd", j)])
            P.op("dve", lambda e: e.scalar_tensor_tensor(out=tmp[j][:], in0=xt_ap, scalar=rstd[j][:], in1=gs_ap,
                                                         op0=ALU.mult, op1=ALU.mult),
                 reads=[xt_key, ("rstd", j), "modsA"], writes=[("ntmp", j)])
            P.op("pool", lambda e: e.tensor_tensor(out=hb[j][:], in0=tmp[j][:], in1=sh_ap, op=ALU.add),
                 reads=[("ntmp", j), "modsA"], writes=[("hb", j)])

        def norm_back(i, bufs):
            hb = bufs[5]
            j = i % 2
            for k in range(8):
                P.op("pe", lambda e, k=k: e.transpose(out=pbt[:, k, :], in_=hb[j][:, k * 128:(k + 1) * 128],
                                                      identity=ident[:]),
                     reads=[("hb", j), "consts"], writes=["pbt"])
            P.op("act", lambda e: e.copy(out=hT[:, :, i * 128:(i + 1) * 128], in_=pbt[:]),
                 reads=["pbt"], writes=[("hT", i // 4)])

        with contextlib.ExitStack() as st:
            T = lambda name, shape, dt: st.enter_context(nc.sbuf_tensor(name, shape, dt))
            cf = T("cf", [128, 128], F32)
            mf = T("mf", [128, 4, 512], F32)
            c_sb = T("c_sb", [128, 8], F32)
            c_act = T("c_act", [128, 8], F32)
            c_rep = T("c_rep", [128, 8, 128], BF16)
            wa = [T("wa%d" % i, [128, 8, 512], BF16) for i in range(2)]
            bb = [T("bb%d" % i, [128, 512], F32) for i in range(2)]
            g_bc = [T("g_bc%d" % i, [128, D], F32) for i in range(2)]
            modsA = T("modsA", [128, 2, D], F32)
            mods = T("mods", [128, 4, D], F32)
            gate1, sh2, gs2, gate2 = (mods[:, j, :] for j in range(4))
            sh1, gs1 = modsA[:, 0, :], modsA[:, 1, :]
            xt = [T("xt%d" % i, [128, D], F32) for i in range(3)]
            nbufs = (T("junk", [128, D], BF16),
                     [T("ss%d" % i, [128, 1], F32) for i in range(2)],
                     [T("sq%d" % i, [128, 1], F32) for i in range(2)],
                     [T("rstd%d" % i, [128, 1], F32) for i in range(2)],
                     [T("ntmp%d" % i, [128, D], F32) for i in range(2)],
                     [T("hb%d" % i, [128, D], BF16) for i in range(2)])

            def const_mat(dst, pattern, cmp, base, cm):
                P.op("pool", lambda e: e.memset(cf[:], 1.0), writes=["cf"])
                if pattern is not None:
                    P.op("pool", lambda e: e.affine_select(out=cf[:], in_=cf[:], pattern=pattern, compare_op=cmp,
                                                           fill=0.0, base=base, channel_multiplier=cm),
                         reads=["cf"], writes=["cf"])
                P.op("dve", lambda e: e.tensor_copy(out=dst[:], in_=cf[:]), reads=["cf"], writes=["consts"])

            const_mat(ident, [[-1, 128]], ALU.is_equal, 0, 1)
            const_mat(tri, [[-1, 128]], ALU.is_ge, 0, 1)
            const_mat(ub, [[1, 128]], ALU.is_gt, 0, -1)
            const_mat(ones_b, None, None, 0, 0)
            P.op("pool", lambda e: e.memset(mf[:], 1.0), writes=["mf"])
            for i in range(4):
                P.op("pool", lambda e, i=i: e.affine_select(out=mf[:, i, :], in_=mf[:, i, :], pattern=[[1, 512]],
                                                            compare_op=ALU.is_gt, fill=0.0, base=-128 * i,
                                                            channel_multiplier=-1), reads=["mf"], writes=["mf"])
            P.op("dve", lambda e: e.tensor_copy(out=maskb[:], in_=mf[:]), reads=["mf"], writes=["consts"])
            P.op("pool", lambda e: e.memset(cf[:], 1.0), reads=["consts"], writes=["cf"])

            dma("sp", c_sb[:], c_col, "c", writes=["c_sb"])
            dma("sp", g_bc[0][:], n1g[0, :].partition_broadcast(128), "g0", writes=["g_bc0"])
            dma("sp", g_bc[1][:], n2g[0, :].partition_broadcast(128), "g1", writes=["g_bc1"])
            P.op("act", lambda e: e.activation(out=c_act[:], in_=c_sb[:], func=AF.Silu), reads=["c_sb"], writes=["c_act"])
            for k in range(8):
                P.op("dve", lambda e, k=k: e.tensor_scalar(out=c_rep[:, k, :], in0=cf[:], scalar1=c_act[:, k:k + 1],
                                                           scalar2=None, op0=ALU.mult),
                     reads=["cf", "c_act"], writes=["c_rep"])
            mod_dst = [sh1, gs1, gate1, sh2, gs2, gate2]
            for g in range(12):
                j = g % 2
                m, half = divmod(g, 2)
                dma("pool", wa[j][:], wview(w_ada, g * 512, 512), ("wa", j), writes=[("wa", j)])
                dma("sp", bb[j][:], b_ada[0, g * 512:(g + 1) * 512].partition_broadcast(128), ("bb", j),
                    writes=[("bb", j)])
                mm_group(pb[j][:], [(c_rep[:, k, :], wa[j][:, k, :]) for k in range(8)],
                         reads=["c_rep", ("wa", j)], wkey=("pb", j))
                dst = mod_dst[m][:, half * 512:(half + 1) * 512]
                mkey = "modsA" if m < 2 else "mods"
                P.op("dve", lambda e, dst=dst, j=j: e.tensor_tensor(out=dst, in0=pb[j][:], in1=bb[j][:], op=ALU.add),
                     reads=[("pb", j), ("bb", j)], writes=[mkey])
                if m in (1, 4):
                    gsrc = g_bc[0 if m == 1 else 1][:, half * 512:(half + 1) * 512]
                    P.op("dve", lambda e, dst=dst, gsrc=gsrc: e.scalar_tensor_tensor(
                        out=dst, in0=dst, scalar=1.0, in1=gsrc, op0=ALU.add, op1=ALU.mult),
                         reads=[mkey, "g_bc0", "g_bc1"], writes=[mkey])
            dma("sp", mods_d, mods[:], "modst", reads=["mods"])
            for i in range(NTILE):
                j = i % 3
                dma("sp", xt[j][:], x[i * 128:(i + 1) * 128, :], ("xt", j), writes=[("xt", j)])
                norm_front(i, xt[j][:], ("xt", j), gs1, sh1, nbufs)
                if i >= 1:
                    norm_back(i - 1, nbufs)
            norm_back(NTILE - 1, nbufs)
            P.emit("A")

        with contextlib.ExitStack() as wst:
            wga = wst.enter_context(nc.sbuf_tensor("wga", [128, 8, D], BF16))
            wpa = wst.enter_context(nc.sbuf_tensor("wpa_s", [128, 8, D], BF16))
            with contextlib.ExitStack() as st:
                T = lambda name, shape, dt: st.enter_context(nc.sbuf_tensor(name, shape, dt))
                wq = [T("wq%d" % i, [128, 8, 128], BF16) for i in range(2)]
                wk = [T("wk%d" % i, [128, 8, 128], BF16) for i in range(2)]
                wv = T("wv", [128, 8, 256], BF16)
                qT = [T("qT%d" % i, [128, S], BF16) for i in range(2)]
                kT = [T("kT%d" % i, [128, S], BF16) for i in range(2)]
                v_sb = T("v_sb", [128, NTILE, 256], BF16)
                q2 = [T("q2_%d" % i, [128, 512], BF16) for i in range(2)]
                sd = [T("sd%d" % i, [128, 512], F32) for i in range(2)]
                rs = [T("rs%d" % i, [128, 512], F32) for i in range(2)]
                e_b = [T("e_b%d" % i, [128, 1024], F32) for i in range(3)]
                L_b = [T("L_b%d" % i, [128, 1024], BF16) for i in range(2)]
                g_b = [T("g_b%d" % i, [128, 1024], F32) for i in range(2)]
                w_b = [T("w_b%d" % i, [128, 1024], BF16) for i in range(2)]
                o_sb = T("o_sb", [128, 1024], BF16)
                gq_raw = T("gq_raw", [128, 1], F32)
                gq = T("gq", [128, 1], F32)
                gk = T("gk", [128, 1], F32)
                z2 = [PP[0], PP[1]]
                C2, O2 = PP[2], PP[3]
                zb = [PP[0][:, 0:512], PP[1][:, 0:512]]
                sb_ = [PP[2][:, 0:512], PP[3][:, 0:512]]
                sbk = ["C", "O"]

                dma("sp", gq_raw[:], qg_col, "gq", writes=["gq_raw"])
                dma("sp", gk[:], kg_col, "gk", writes=["gk"])
                P.op("act", lambda e: e.mul(gq[:], gq_raw[:], 128.0 ** -0.5), reads=["gq_raw"], writes=["gq"])

                pcount = [0]

                def qk_proj(W, wkey, gcol, gkey, dst, dkey):
                    for tc in range(NCH):
                        t = pcount[0] % 2
                        pcount[0] += 1
                        sl = slice(tc * 512, (tc + 1) * 512)
                        mm_group(zb[t][:], [(W[:, k, :], hT[:, k, sl]) for k in range(8)],
                                 reads=[wkey, ("hT", tc)], wkey=("z", t))
                        P.op("act", lambda e, t=t: e.activation(out=q2[t][:], in_=zb[t][:], func=AF.Square),
                             reads=[("z", t)], writes=[("q2", t)])
                        P.op("pe", lambda e, t=t: e.matmul(sb_[t][:], lhsT=ones_b[:], rhs=q2[t][:], start=True, stop=True),
                             reads=[("q2", t)], writes=[sbk[t]])
                        P.op("act", lambda e, t=t: e.activation(out=sd[t][:], in_=sb_[t][:], func=AF.Ln, scale=1.0 / 128,
                                                                bias=EPS), reads=[sbk[t]], writes=[("sd", t)])
                        P.op("act", lambda e, t=t: e.activation(out=rs[t][:], in_=sd[t][:], func=AF.Exp, scale=-0.5),
                             reads=[("sd", t)], writes=[("rs", t)])
                        P.op("dve", lambda e, t=t, sl=sl: e.scalar_tensor_tensor(
                            out=dst[:, sl], in0=zb[t][:], scalar=gcol[:], in1=rs[t][:], op0=ALU.mult, op1=ALU.mult),
                             reads=[("z", t), ("rs", t), gkey], writes=[(dkey, tc)])

                for p in range(4):
                    for s in range(2):
                        h = 2 * p + s
                        dma("pool", wq[s][:], wview(w_in, h * 128, 128), ("wq", s), writes=[("wq", s)])
                        dma("pool", wk[s][:], wview(w_in, 1024 + h * 128, 128), ("wk", s), writes=[("wk", s)])
                    dma("pool", wv[:], wview(w_in, 2048 + p * 256, 256), "wv", writes=["wv"])
                    if p == 1:
                        dma("pool", wga[:], wview(w_in, 6144, D), "wga")
                        dma("pool", wpa[:], wpa_d.rearrange("(k p) c -> p k c", p=128), "wpa")
                    for s in range(2):
                        qk_proj(wq[s], ("wq", s), gq, "gq", qT[s], ("qT", s))
                        qk_proj(wk[s], ("wk", s), gk, "gk", kT[s], ("kT", s))
                    for i in range(NTILE):
                        t = i % 2
                        mm_group(sb_[t][:, 0:256], [(hT[:, k, i * 128:(i + 1) * 128], wv[:, k, :]) for k in range(8)],
                                 reads=["wv", ("hT", i // 4)], wkey=sbk[t])
                        if t == 0:
                            P.op("dve", lambda e, i=i, t=t: e.tensor_copy(out=v_sb[:, i, :], in_=sb_[t][:, 0:256]),
                                 reads=[sbk[t]], writes=[("v", i)])
                        else:
                            P.op("act", lambda e, i=i, t=t: e.copy(out=v_sb[:, i, :], in_=sb_[t][:, 0:256]),
                                 reads=[sbk[t]], writes=[("v", i)])

                    steps = [(c, kb) for c in range(NCH) for kb in range(4 * c + 3, -1, -1)]
                    n = len(steps)
                    H = [slice(0, 512), slice(512, 1024)]

                    def QK(r):
                        c, kb = steps[r]
                        for s in range(2):
                            P.op("pe", lambda e, s=s: e.matmul(z2[r % 2][:, H[s]], lhsT=kT[s][:, kb * 128:(kb + 1) * 128],
                                                               rhs=qT[s][:, c * 512:(c + 1) * 512], start=True, stop=True),
                                 reads=[(("kT", s), kb // 4), (("qT", s), c)], writes=[("z", r % 2)])

                    def Bq(r):
                        c, kb = steps[r]
                        P.op("act", lambda e: e.activation(out=e_b[r % 3][:], in_=z2[r % 2][:], func=AF.Exp),
                             reads=[("z", r % 2)], writes=[("e", r % 3)])
                        if kb >= 4 * c:
                            i = kb - 4 * c
                            for s in range(2):
                                P.op("dve", lambda e, s=s: e.tensor_tensor(out=e_b[r % 3][:, H[s]], in0=e_b[r % 3][:, H[s]],
                                                                           in1=maskb[:, i, :], op=ALU.mult),
                                     reads=[("e", r % 3)], writes=[("e", r % 3)])

                    def Cq(r):
                        c, kb = steps[r]
                        P.op("act", lambda e: e.activation(out=L_b[r % 2][:], in_=e_b[r % 3][:], func=AF.Ln, bias=1.0),
                             reads=[("e", r % 3)], writes=[("L", r % 2)])
                        for s in range(2):
                            P.op("pe", lambda e, s=s: e.matmul(C2[:, H[s]], lhsT=tri[:], rhs=L_b[r % 2][:, H[s]],
                                                               start=(kb == 4 * c + 3), stop=(kb == 0)),
                                 reads=[("L", r % 2)], writes=["C"])

                    def Eq(r):
                        P.op("act", lambda e: e.activation(out=g_b[r % 2][:], in_=C2[:], func=AF.Exp, scale=-1.0),
                             reads=["C"], writes=[("g", r % 2)])

                    def Tail(r):
                        c, kb = steps[r]
                        if kb != 0:
                            for s in range(2):
                                P.op("pe", lambda e, s=s: e.matmul(C2[:, H[s]], lhsT=ub[:], rhs=L_b[r % 2][:, H[s]],
                                                                   start=False, stop=False),
                                     reads=[("L", r % 2)], writes=["C"])
                        P.op("dve", lambda e: e.tensor_tensor(out=w_b[r % 2][:], in0=e_b[r % 3][:], in1=g_b[r % 2][:],
                                                              op=ALU.mult),
                             reads=[("e", r % 3), ("g", r % 2)], writes=[("w", r % 2)])
                        for s in range(2):
                            P.op("pe", lambda e, s=s: e.matmul(O2[:, H[s]], lhsT=v_sb[:, kb, s * 128:(s + 1) * 128],
                                                               rhs=w_b[r % 2][:, H[s]],
                                                               start=(kb == 4 * c + 3), stop=(kb == 0)),
                                 reads=[("w", r % 2), ("v", kb)], writes=["O"])
                        if kb == 0:
                            P.op("dve", lambda e: e.tensor_copy(out=o_sb[:], in_=O2[:]), reads=["O"], writes=["o_sb"])
                            for s in range(2):
                                dma("sp", oT_d[2 * p + s, :, c * 512:(c + 1) * 512], o_sb[:, H[s]], ("ost", s),
                                    reads=["o_sb"])

                    QK(0)
                    for r in range(n + 1):
                        if r + 1 < n:
                            QK(r + 1)
                        if r < n:
                            Bq(r)
                        if r >= 1:
                            Eq(r - 1)
                        if r < n:
                            Cq(r)
                        if r >= 1:
                            Tail(r - 1)
                P.emit("B")

            with contextlib.ExitStack() as st:
                T = lambda name, shape, dt: st.enter_context(nc.sbuf_tensor(name, shape, dt))
                oTc = [T("oTc%d" % i, [128, 8, 512], BF16) for i in range(2)]
                sga = [T("sga%d" % i, [128, 512], F32) for i in range(2)]
                m1 = [T("m1_%d" % i, [128, 512], F32) for i in range(2)]
                wgrA = T("wgrA", [128, 8, 1536], BF16)
                gst = [T("gst%d" % i, [128, 512], F32) for i in range(4)]
                gcnt = [0]
                dma("pool", wgrA[:], wview(w_in, 4608, 1536), "wgrA", writes=["wgrA"])
                def load_oT(c):
                    dma("sp", oTc[c % 2][:], oT_d[:, :, c * 512:(c + 1) * 512].rearrange("h p t -> p h t"),
                        ("oTc", c % 2), writes=[("oTc", c % 2)])

                load_oT(0)
                for c in range(NCH):
                    cj = c % 2
                    sl = slice(c * 512, (c + 1) * 512)
                    if c + 1 < NCH:
                        load_oT(c + 1)
                    for d in range(8):
                        dj = d % 2
                        dsl = slice(d * 128, (d + 1) * 128)
                        mm_group(pb[dj][:], [(wga[:, k, dsl], hT[:, k, sl]) for k in range(8)],
                                 reads=[("hT", c)], wkey=("pb", dj))
                        P.op("act", lambda e, dj=dj: e.activation(out=sga[dj][:], in_=pb[dj][:], func=AF.Sigmoid),
                             reads=[("pb", dj)], writes=[("sga", dj)])
                        mm_group(pb[2 + dj][:], [(wpa[:, a, dsl], oTc[cj][:, a, :]) for a in range(8)],
                                 reads=[("oTc", cj)], wkey=("pb", 2 + dj))
                        P.op("dve", lambda e, dj=dj: e.tensor_tensor(out=m1[dj][:], in0=pb[2 + dj][:], in1=sga[dj][:],
                                                                    op=ALU.mult),
                             reads=[("pb", 2 + dj), ("sga", dj)], writes=[("m1", dj)])
                        dma("sp", mA_d[d, :, sl], m1[dj][:], ("m1st", dj), reads=[("m1", dj)])
                    for n_ in range(LRU_N):
                        t = 4 + gcnt[0] % 3
                        gj = gcnt[0] % 4
                        gcnt[0] += 1
                        mm_group(pb[t][:], [(wgrA[:, k, n_ * 128:(n_ + 1) * 128], hT[:, k, sl]) for k in range(8)],
                                 reads=["wgrA", ("hT", c)], wkey=("pb", t))
                        P.op("act", lambda e, t=t, gj=gj: e.activation(out=gst[gj][:], in_=pb[t][:],
                                                                       func=AF.Gelu_apprx_tanh),
                             reads=[("pb", t)], writes=[("gst", gj)])
                        dma("sp", gl_d[n_, :, sl], gst[gj][:], ("gstst", gj), reads=[("gst", gj)])
                P.emit("D1")

        with contextlib.ExitStack() as wst:
            wgb = wst.enter_context(nc.sbuf_tensor("wgb", [128, 8, D], BF16))
            wpl = wst.enter_context(nc.sbuf_tensor("wpl_s", [128, LRU_N, D], BF16))
            with contextlib.ExitStack() as st:
                T = lambda name, shape, dt: st.enter_context(nc.sbuf_tensor(name, shape, dt))
                TP = 1024
                NQ = S // TP
                X1 = [T("X1_%d" % i, [128, TP + 4], F32) for i in range(3)]
                X2 = [T("X2_%d" % i, [128, TP], F32) for i in range(3)]
                X3 = [T("X3_%d" % i, [128, TP], F32) for i in range(2)]
                X4 = [T("X4_%d" % i, [128, TP], F32) for i in range(2)]
                X5 = [T("X5_%d" % i, [128, TP], F32) for i in range(2)]
                GL = [T("GL_%d" % i, [128, TP], F32) for i in range(3)]
                yb = [T("yb%d" % i, [128, TP], BF16) for i in range(2)]
                hc = T("hc", [128, 2], F32)
                wxr = [T("wxr%d" % i, [128, 8, 128], BF16) for i in range(3)]
                wrg = [T("wrg%d" % i, [128, 128], F32) for i in range(3)]
                wig = [T("wig%d" % i, [128, 128], F32) for i in range(3)]
                cw = T("cw", [128, 4, LRU_N], F32)
                cb = T("cb", [128, LRU_N], F32)
                brg = T("brg", [128, LRU_N], F32)
                big = T("big", [128, LRU_N], F32)
                lam = T("lam", [128, LRU_N], F32)
                cA = T("cA", [128, LRU_N], F32)
                cA2 = T("cA2", [128, LRU_N], F32)
                dma("sp", cw[:], cw_col, "cw", writes=["cols"])
                dma("sp", cb[:], cb_col, "cb", writes=["cols"])
                dma("sp", brg[:], brg_col, "brg", writes=["brg"])
                dma("sp", big[:], big_col, "big", writes=["big"])
                dma("sp", lam[:], lam_col, "lam", writes=["lam"])
                P.op("act", lambda e: e.mul(brg[:], brg[:], -1.0), reads=["brg"], writes=["brg"])
                P.op("act", lambda e: e.mul(big[:], big[:], -1.0), reads=["big"], writes=["big"])
                P.op("act", lambda e: e.activation(out=cA[:], in_=lam[:], func=AF.Exp, scale=-1.0), reads=["lam"], writes=["cA"])
                P.op("act", lambda e: e.activation(out=cA2[:], in_=cA[:], func=AF.Ln, bias=1.0), reads=["cA"], writes=["cA2"])
                P.op("act", lambda e: e.mul(cA[:], cA2[:], -8.0), reads=["cA2"], writes=["cA"])
                P.op("act", lambda e: e.mul(cA2[:], cA2[:], -16.0), reads=["cA2", "cA"], writes=["cA2"])
                units = [(n_, q) for n_ in range(LRU_N) for q in range(NQ)]
                bank = [0]

                def nb():
                    t = bank[0] % 7
                    bank[0] += 1
                    return t

                def load_w(n_):
                    j = n_ % 3
                    dma("pool", wxr[j][:], wview(w_in, 3072 + n_ * 128, 128), ("wxr", j), writes=[("wxr", j)])
                    dma("sp", wrg[j][:], w_rg[n_], ("wrg", j), writes=[("wrg", j)])
                    dma("sp", wig[j][:], w_ig[n_], ("wig", j), writes=[("wig", j)])

                def S1(u):
                    n_, q = units[u]
                    b, j = u % 3, n_ % 3
                    tok0 = q * TP
                    if q == 0:
                        if n_ + 1 < LRU_N:
                            load_w(n_ + 1)
                        P.op("pool", lambda e: e.memset(X1[b][:, 0:4], 0.0), writes=[("X1h", b)])
                    if u == 3:
                        dma("pool", wgb[:], wview(w_in, 7168, D), "wgb")
                    if u == 5:
                        dma("pool", wpl[:], wpl_d.rearrange("(k p) c -> p k c", p=128), "wpl")
                    dma("sp", GL[b][:], gl_d[n_, :, tok0:tok0 + TP], ("GL", b), writes=[("GL", b)])
                    for half in range(2):
                        t = nb()
                        tok = tok0 + half * 512
                        mm_group(pb[t][:], [(wxr[j][:, k, :], hT[:, k, tok:tok + 512]) for k in range(8)],
                                 reads=[("wxr", j), ("hT", tok // 512)], wkey=("pb", t))
                        P.op("act", lambda e, t=t, half=half: e.copy(out=X1[b][:, 4 + half * 512:4 + (half + 1) * 512],
                                                                     in_=pb[t][:]),
                             reads=[("pb", t)], writes=[("X1", b)])
                    if q < NQ - 1:
                        P.op("pool", lambda e: e.tensor_copy(out=X1[(u + 1) % 3][:, 1:4], in_=X1[b][:, TP + 1:TP + 4]),
                             reads=[("X1", b)], writes=[("X1h", (u + 1) % 3)])
                    P.op("dve", lambda e: e.tensor_scalar(out=X2[b][:], in0=X1[b][:, 4:4 + TP], scalar1=cw[:, 3, n_:n_ + 1],
                                                          scalar2=cb[:, n_:n_ + 1], op0=ALU.mult, op1=ALU.add),
                         reads=[("X1", b), "cols"], writes=[("X2", b)])
                    for kk in range(3):
                        P.op("dve", lambda e, kk=kk: e.scalar_tensor_tensor(
                            out=X2[b][:], in0=X1[b][:, 1 + kk:1 + kk + TP], scalar=cw[:, kk, n_:n_ + 1], in1=X2[b][:],
                            op0=ALU.mult, op1=ALU.add), reads=[("X1", b), ("X1h", b), ("X2", b), "cols"],
                             writes=[("X2", b)])

                def S2(u):
                    n_, q = units[u]
                    b, j, a3 = u % 2, n_ % 3, u % 3
                    tok0 = q * TP
                    g = a3
                    for (wg, wgk, nbcol, nbk, dstB, dk) in ((wrg, "wrg", brg, "brg", X3, "X3"),
                                                          (wig, "wig", big, "big", X5, "X5")):
                        for half in range(2):
                            t = nb()
                            sl = slice(half * 512, (half + 1) * 512)
                            P.op("pe", lambda e, t=t, sl=sl, wg=wg: e.matmul(pb[t][:], lhsT=wg[j][:], rhs=X2[a3][:, sl],
                                                                            start=True, stop=True),
                                 reads=[(wgk, j), ("X2", a3)], writes=[("pb", t)])
                            P.op("act", lambda e, t=t, sl=sl, nbcol=nbcol: e.activation(
                                out=X4[b][:, sl], in_=pb[t][:], func=AF.Exp, scale=-1.0, bias=nbcol[:, n_:n_ + 1]),
                                 reads=[("pb", t), nbk], writes=[("X4", b)])
                        P.op("act", lambda e: e.activation(out=X4[b][:], in_=X4[b][:], func=AF.Ln, bias=1.0),
                             reads=[("X4", b)], writes=[("X4", b)])
                        P.op("act", lambda e, dstB=dstB: e.activation(out=dstB[b][:], in_=X4[b][:], func=AF.Exp,
                                                                      scale=-1.0),
                             reads=[("X4", b)], writes=[(dk, b)])
                    P.op("pool", lambda e: e.tensor_tensor(out=X5[b][:], in0=X5[b][:], in1=X2[a3][:], op=ALU.mult),
                         reads=[("X5", b), ("X2", a3)], writes=[("X5", b)])
                    P.op("act", lambda e: e.activation(out=X4[b][:], in_=X3[b][:], func=AF.Exp, scale=cA[:, n_:n_ + 1]),
                         reads=[("X3", b), "cA"], writes=[("X4", b)])
                    P.op("act", lambda e: e.activation(out=X3[b][:], in_=X3[b][:], func=AF.Exp, scale=cA2[:, n_:n_ + 1]),
                         reads=[("X3", b), "cA2"], writes=[("X3", b)])
                    P.op("act", lambda e: e.activation(out=X3[b][:], in_=X3[b][:], func=AF.Ln, scale=-1.0, bias=1.0),
                         reads=[("X3", b)], writes=[("X3", b)])
                    P.op("act", lambda e: e.activation(out=X3[b][:], in_=X3[b][:], func=AF.Exp, scale=0.5),
                         reads=[("X3", b)], writes=[("X3", b)])
                    P.op("dve", lambda e: e.tensor_tensor(out=X5[b][:], in0=X5[b][:], in1=X3[b][:], op=ALU.mult),
                         reads=[("X5", b), ("X3", b)], writes=[("X5", b)])
                    init = 0.0 if q == 0 else hc[:, (u - 1) % 2:(u - 1) % 2 + 1]
                    P.op("dve", lambda e: e.tensor_tensor_scan(out=X3[b][:], data0=X4[b][:], data1=X5[b][:], initial=init,
                                                               op0=ALU.mult, op1=ALU.add),
                         reads=[("X4", b), ("X5", b), ("X3", b), ("hc", (u - 1) % 2)], writes=[("X3", b)])
                    P.op("dve", lambda e: e.tensor_copy(out=hc[:, u % 2:u % 2 + 1], in_=X3[b][:, TP - 1:TP]),
                         reads=[("X3", b)], writes=[("hc", u % 2)])
                    P.op("pool", lambda e: e.tensor_tensor(out=yb[b][:], in0=GL[g][:], in1=X3[b][:], op=ALU.mult),
                         reads=[("GL", g), ("X3", b)], writes=[("yb", b)])
                    dma("sp", yT_d[n_, :, tok0:tok0 + TP], yb[b][:], ("yst", b), reads=[("yb", b)])

                load_w(0)
                S1(0)
                S1(1)
                for u in range(len(units)):
                    if u + 2 < len(units):
                        S1(u + 2)
                    S2(u)
                P.emit("C")

            with contextlib.ExitStack() as wst2:
                wo = wst2.enter_context(nc.sbuf_tensor("wo_s", [128, 8, D], BF16))
                with contextlib.ExitStack() as st:
                    T = lambda name, shape, dt: st.enter_context(nc.sbuf_tensor(name, shape, dt))
                    yTc = [T("yTc%d" % i, [128, LRU_N, 512], BF16) for i in range(2)]
                    mAd = [T("mAd%d" % i, [128, 512], F32) for i in range(3)]
                    sgb = [T("sgb%d" % i, [128, 512], F32) for i in range(2)]
                    m2 = [T("m2_%d" % i, [128, 512], F32) for i in range(2)]
                    mo = [T("mo%d" % i, [128, 512], BF16) for i in range(2)]
                    dma("pool", wo[:], wo_d.rearrange("(k p) c -> p k c", p=128), "wo")
                    def load_yT(c):
                        dma("sp", yTc[c % 2][:], yT_d[:, :, c * 512:(c + 1) * 512].rearrange("h p t -> p h t"),
                            ("yTc", c % 2), writes=[("yTc", c % 2)])

                    def load_mA(it):
                        c_, d_ = divmod(it, 8)
                        dma("sp", mAd[it % 3][:], mA_d[d_, :, c_ * 512:(c_ + 1) * 512], ("mAd", it % 3),
                            writes=[("mAd", it % 3)])

                    load_yT(0)
                    load_mA(0)
                    load_mA(1)
                    for c in range(NCH):
                        cj = c % 2
                        sl = slice(c * 512, (c + 1) * 512)
                        if c + 1 < NCH:
                            load_yT(c + 1)
                        for d in range(8):
                            dj = d % 2
                            dsl = slice(d * 128, (d + 1) * 128)
                            it = c * 8 + d
                            if it + 2 < NCH * 8:
                                load_mA(it + 2)
                            mm_group(pb[dj][:], [(wgb[:, k, dsl], hT[:, k, sl]) for k in range(8)],
                                     reads=[("hT", c)], wkey=("pb", dj))
                            P.op("act", lambda e, dj=dj: e.activation(out=sgb[dj][:], in_=pb[dj][:], func=AF.Sigmoid),
                                 reads=[("pb", dj)], writes=[("sgb", dj)])
                            mm_group(pb[2 + dj][:], [(wpl[:, a, dsl], yTc[cj][:, a, :]) for a in range(LRU_N)],
                                     reads=[("yTc", cj)], wkey=("pb", 2 + dj))
                            P.op("dve", lambda e, dj=dj: e.tensor_tensor(out=m2[dj][:], in0=pb[2 + dj][:], in1=sgb[dj][:],
                                                                        op=ALU.mult),
                                 reads=[("pb", 2 + dj), ("sgb", dj)], writes=[("m2", dj)])
                            P.op("pool", lambda e, dj=dj, it=it: e.tensor_tensor(out=mo[dj][:], in0=m2[dj][:],
                                                                                     in1=mAd[it % 3][:], op=ALU.add),
                                 reads=[("m2", dj), ("mAd", it % 3)], writes=[("mo", dj)])
                            dma("sp", mT_d[d, :, sl], mo[dj][:], ("most", dj), reads=[("mo", dj)])
                    P.emit("D2")

                with contextlib.ExitStack() as st:
                    T = lambda name, shape, dt: st.enter_context(nc.sbuf_tensor(name, shape, dt))
                    mTc = [T("mTc%d" % i, [128, 8, 512], BF16) for i in range(2)]
                    xt = [T("xtD%d" % i, [128, D], F32) for i in range(2)]
                    x1t = [T("x1t%d" % i, [128, D], F32) for i in range(3)]
                    gtmp = [T("gtmp%d" % i, [128, 512], F32) for i in range(2)]
                    mods = T("modsD", [128, 4, D], F32)
                    gate1, sh2, gs2, gate2 = (mods[:, j, :] for j in range(4))
                    dma("sp", mods[:], mods_d, "modld", writes=["mods", "modsA"])
                    nbufs = (T("junkD", [128, D], BF16),
                             [T("ssD%d" % i, [128, 1], F32) for i in range(2)],
                             [T("sqD%d" % i, [128, 1], F32) for i in range(2)],
                             [T("rstdD%d" % i, [128, 1], F32) for i in range(2)],
                             [T("ntmpD%d" % i, [128, D], F32) for i in range(2)],
                             [T("hbD%d" % i, [128, D], BF16) for i in range(2)])
                    def load_mT(c):
                        dma("sp", mTc[c % 2][:], mT_d[:, :, c * 512:(c + 1) * 512].rearrange("h p t -> p h t"),
                            ("mTc", c % 2), writes=[("mTc", c % 2)])

                    load_mT(0)
                    for c in range(NCH):
                        cj = c % 2
                        sl = slice(c * 512, (c + 1) * 512)
                        if c + 1 < NCH:
                            load_mT(c + 1)
                        for jt in range(4):
                            i = 4 * c + jt
                            ij = i % 2
                            dma("sp", xt[ij][:], x[i * 128:(i + 1) * 128, :], ("xtD", ij), writes=[("xtD", ij)])
                            for half in range(2):
                                hs = slice(half * 512, (half + 1) * 512)
                                bi = 2 * (i % 2) + half
                                bk = pb[bi]
                                mm_group(bk[:], [(mTc[cj][:, d, jt * 128:(jt + 1) * 128], wo[:, d, hs]) for d in range(8)],
                                         reads=[("mTc", cj)], wkey=("pb", bi))
                                P.op("dve", lambda e, hs=hs, bk=bk, half=half: e.tensor_tensor(
                                    out=gtmp[half][:], in0=bk[:], in1=gate1[:, hs], op=ALU.mult),
                                     reads=[("pb", bi), "mods"], writes=[("gtmp", half)])
                                P.op("dve", lambda e, hs=hs, ij=ij, i3=i % 3, half=half: e.tensor_tensor(
                                    out=x1t[i3][:, hs], in0=gtmp[half][:], in1=xt[ij][:, hs], op=ALU.add),
                                     reads=[("gtmp", half), ("xtD", ij)], writes=[("x1t", i % 3)])
                            dma("sp", x1_d[i * 128:(i + 1) * 128, :], x1t[i % 3][:], ("x1st", i % 3),
                                reads=[("x1t", i % 3)])
                            if i >= 1:
                                norm_front(i - 1, x1t[(i - 1) % 3][:], ("x1t", (i - 1) % 3), gs2, sh2, nbufs)
                            if i >= 2:
                                norm_back(i - 2, nbufs)
                    norm_front(NTILE - 1, x1t[(NTILE - 1) % 3][:], ("x1t", (NTILE - 1) % 3), gs2, sh2, nbufs)
                    norm_back(NTILE - 2, nbufs)
                    norm_back(NTILE - 1, nbufs)
                    P.emit("D3")

        with contextlib.ExitStack() as wst:
            w2 = wst.enter_context(nc.sbuf_tensor("w2", [128, FFN_N, D], BF16))
            with contextlib.ExitStack() as st:
                T = lambda name, shape, dt: st.enter_context(nc.sbuf_tensor(name, shape, dt))
                wg_ = [T("wg%d" % i, [128, 8, 128], BF16) for i in range(2)]
                wu_ = [T("wu%d" % i, [128, 8, 128], BF16) for i in range(2)]
                sg = [T("sg%d" % i, [128, 512], F32) for i in range(2)]
                ab = [T("ab%d" % i, [128, S], BF16) for i in range(2)]
                cnt = 0
                for f in range(FFN_N):
                    j = f % 2
                    dma("pool", wg_[j][:], wview(wf1_d, f * 128, 128), ("wg", j), writes=[("wg", j)])
                    dma("pool", wu_[j][:], wview(wf1_d, 2816 + f * 128, 128), ("wu", j), writes=[("wu", j)])
                    if f in (2, 3):
                        hf = f - 2
                        dma("pool", w2[:, hf * 11:(hf + 1) * 11, :],
                            wf2_d[hf * 1408:(hf + 1) * 1408, :].rearrange("(k p) c -> p k c", p=128), ("w2", hf))
                    for tc in range(NCH):
                        t = cnt % 3
                        cnt += 1
                        sl = slice(tc * 512, (tc + 1) * 512)
                        bg, bu = pb[2 * t], pb[2 * t + 1]
                        mm_group(bg[:], [(wg_[j][:, k, :], hT[:, k, sl]) for k in range(8)],
                                 reads=[("wg", j), ("hT", tc)], wkey=("pb", 2 * t))
                        mm_group(bu[:], [(wu_[j][:, k, :], hT[:, k, sl]) for k in range(8)],
                                 reads=[("wu", j), ("hT", tc)], wkey=("pb", 2 * t + 1))
                        tj = cnt % 2
                        P.op("act", lambda e, bg=bg, tj=tj: e.activation(out=sg[tj][:], in_=bg[:], func=AF.Silu),
                             reads=[("pb", 2 * t)], writes=[("sg", tj)])
                        P.op("dve", lambda e, bu=bu, tj=tj, sl=sl, j=j: e.tensor_tensor(out=ab[j][:, sl], in0=bu[:],
                                                                                      in1=sg[tj][:], op=ALU.mult),
                             reads=[("pb", 2 * t + 1), ("sg", tj)], writes=[("ab", j)])
                    dma("sp", aT_d[f], ab[j][:], ("ast", j), reads=[("ab", j)])
                P.emit("E")

            with contextlib.ExitStack() as st:
                T = lambda name, shape, dt: st.enter_context(nc.sbuf_tensor(name, shape, dt))
                ac = [T("ac%d" % i, [128, FFN_N, 512], BF16) for i in range(2)]
                x1t = [T("x1F%d" % i, [128, D], F32) for i in range(2)]
                ot = [T("ot%d" % i, [128, D], F32) for i in range(2)]
                gtmp = [T("gtmpF%d" % i, [128, 512], F32) for i in range(2)]
                mods = T("modsF", [128, 4, D], F32)
                gate1, sh2, gs2, gate2 = (mods[:, j, :] for j in range(4))
                dma("sp", mods[:], mods_d, "modld", writes=["mods"])
                cnt = 0
                def load_ac(c):
                    dma("sp", ac[c % 2][:], aT_d[:, :, c * 512:(c + 1) * 512].rearrange("f p t -> p f t"),
                        ("ac", c % 2), writes=[("ac", c % 2)])

                load_ac(0)
                for c in range(NCH):
                    cj = c % 2
                    if c + 1 < NCH:
                        load_ac(c + 1)
                    for jt in range(4):
                        i = 4 * c + jt
                        ij = i % 2
                        dma("sp", x1t[ij][:], x1_d[i * 128:(i + 1) * 128, :], ("x1F", ij), writes=[("x1F", ij)])
                        for half in range(2):
                            t = cnt % 4
                            cnt += 1
                            hs = slice(half * 512, (half + 1) * 512)
                            mm_group(pb[t][:], [(ac[cj][:, f, jt * 128:(jt + 1) * 128], w2[:, f, hs]) for f in range(FFN_N)],
                                     reads=[("ac", cj)], wkey=("pb", t))
                            P.op("dve", lambda e, t=t, hs=hs, half=half: e.tensor_tensor(
                                out=gtmp[half][:], in0=pb[t][:], in1=gate2[:, hs], op=ALU.mult),
                                 reads=[("pb", t), "mods"], writes=[("gtmpF", half)])
                            P.op("pool", lambda e, hs=hs, ij=ij, half=half: e.tensor_tensor(
                                out=ot[ij][:, hs], in0=gtmp[half][:], in1=x1t[ij][:, hs], op=ALU.add),
                                 reads=[("gtmpF", half), ("x1F", ij)], writes=[("ot", ij)])
                        dma("sp", out[i * 128:(i + 1) * 128, :], ot[ij][:], ("ost", ij), reads=[("ot", ij)])
                P.emit("F")
    return nc


def _prep_inputs(inputs):
    f = lambda a: np.ascontiguousarray(np.asarray(a, dtype=np.float32))
    col = lambda v: f(np.asarray(v).reshape(-1, 128).T)
    shared = {
        "w_ada": f(inputs["w_ada"][0]),
        "b_ada": f(inputs["b_ada"][0].reshape(1, -1)),
        "n1g": f(inputs["norm1_g"][0].reshape(1, -1)),
        "n2g": f(inputs["norm2_g"][0].reshape(1, -1)),
        "w_in": f(inputs["w_in"][0]),
        "qg_col": f(inputs["q_norm_g"][0].reshape(128, 1)),
        "kg_col": f(inputs["k_norm_g"][0].reshape(128, 1)),
        "cw_col": f(np.asarray(inputs["conv_w"][0]).reshape(4, LRU_N, 128).transpose(2, 0, 1)),
        "cb_col": col(inputs["conv_b"][0]),
        "brg_col": col(inputs["b_rg"][0]),
        "big_col": col(inputs["b_ig"][0]),
        "lam_col": col(inputs["lru_lambda"][0]),
        "w_rg": f(inputs["w_rg"][0]),
        "w_ig": f(inputs["w_ig"][0]),
        "wpa": f(inputs["w_proj_attn"][0]),
        "wpl": f(inputs["w_proj_lru"][0]),
        "wo": f(inputs["w_out"][0]),
        "wf1": f(inputs["w_ffn_in"][0]),
        "wf2": f(inputs["w_ffn_out"][0]),
    }
    xs = np.asarray(inputs["x"], dtype=np.float32)
    cs = np.asarray(inputs["c"], dtype=np.float32)
    in_maps = []
    for b in range(8):
        m = dict(shared)
        m["x"] = np.ascontiguousarray(xs[b])
        m["c_col"] = col(cs[b])
        in_maps.append(m)
    return in_maps


def kernel(**inputs):
    nc = build_program()
    in_maps = _prep_inputs(inputs)
    res = run_bass_kernel_spmd(nc, in_maps, core_ids=list(range(8)))
    return np.stack([np.asarray(r["out"], dtype=np.float32) for r in res.results], axis=0)
```

```python
import contextlib
import numpy as np
import concourse.bass as bass
import concourse.mybir as mybir
from concourse.bass_utils import run_bass_kernel_spmd

F32 = mybir.dt.float32
BF16 = mybir.dt.bfloat16
AF = mybir.ActivationFunctionType
ALU = mybir.AluOpType

S = 4096
D = 1024
NTILE = 32
NCH = 8
LRU_N = 12
FFN_N = 22
EPS = 1e-6
ENGS = ("pe", "act", "dve", "pool", "sp")
NDSEM = 26
NSSEM = 12


class Prog:
    def __init__(self, nc, st):
        self.nc = nc
        self.esem = {e: st.enter_context(nc.semaphore("s_" + e)) for e in ENGS if e != "sp"}
        self.dpool = ([st.enter_context(nc.semaphore("d%d" % i)) for i in range(NDSEM)]
                      + [st.enter_context(nc.semaphore("w%d" % i)) for i in range(NSSEM)])
        self.dbase = [0] * (NDSEM + NSSEM)
        self.msbase = {e: 0 for e in ENGS}
        self.ops = []

    def op(self, eng, fn, reads=(), writes=(), dma_key=None):
        self.ops.append(dict(eng=eng, fn=fn, reads=tuple(reads), writes=tuple(writes),
                             dma_key=dma_key))

    def emit(self, name):
        nc = self.nc
        ops = self.ops
        self.ops = []
        eng_count = {e: 0 for e in ENGS}
        dma_count = {}
        dma_slot = {}
        n_hw, n_sw, key_q = [0], [0], {}
        last_writer = {}
        readers = {}
        clock = {e: {} for e in ENGS}
        for i, o in enumerate(ops):
            e = o["eng"]
            deps = set()
            for b in o["reads"]:
                if b in last_writer:
                    deps.add(last_writer[b])
            for b in o["writes"]:
                if b in last_writer:
                    deps.add(last_writer[b])
                for r in readers.get(b, ()):
                    deps.add(r)
            deps.discard(i)
            need = {}
            ck = clock[e]
            for d in deps:
                od = ops[d]
                dim, idx = od["dim"], od["idx"]
                if dim == "pe" and e == "pe" and o["dma_key"] is None:
                    continue
                if ck.get(dim, 0) >= idx:
                    continue
                if need.get(dim, (0, None))[0] < idx:
                    need[dim] = (idx, d)
            for dim, (idx, d) in need.items():
                for k, v in ops[d]["vc"].items():
                    if ck.get(k, 0) < v:
                        ck[k] = v
            o["waits"] = [(dim, idx, d) for dim, (idx, d) in need.items()]
            if o["dma_key"] is None:
                eng_count[e] += 1
                o["dim"], o["idx"] = e, eng_count[e]
            else:
                k = o["dma_key"]
                if k not in dma_slot:
                    if e == "pool":
                        dma_slot[k] = NDSEM + n_sw[0]
                        n_sw[0] += 1
                        assert n_sw[0] <= NSSEM, "too many software dma keys"
                    else:
                        dma_slot[k] = n_hw[0]
                        n_hw[0] += 1
                        assert n_hw[0] <= NDSEM, "too many dma keys"
                    key_q[k] = e
                assert key_q[k] == e, "dma key used from two queues"
                dma_count[k] = dma_count.get(k, 0) + 1
                o["dim"], o["idx"] = ("dma", k), dma_count[k]
            vc = dict(ck)
            vc[o["dim"]] = o["idx"]
            o["vc"] = vc
            for b in o["reads"]:
                readers.setdefault(b, []).append(i)
            for b in o["writes"]:
                last_writer[b] = i
                readers[b] = []
        awaited = set()
        for o in ops:
            for dim, idx, d in o["waits"]:
                awaited.add(d)
        mcount = dict(self.msbase)
        for i, o in enumerate(ops):
            if o["dma_key"] is None:
                if i in awaited:
                    mcount[o["eng"]] += 1
                    o["ms"] = mcount[o["eng"]]
                else:
                    o["ms"] = None
        esem, dpool, dbase = self.esem, self.dpool, self.dbase

        def run(ename, eng):
            for o in ops:
                if o["eng"] != ename:
                    continue
                for dim, idx, d in o["waits"]:
                    if isinstance(dim, tuple):
                        sl = dma_slot[dim[1]]
                        eng.wait_ge(dpool[sl], 16 * (dbase[sl] + idx))
                    else:
                        eng.wait_ge(esem[dim], ops[d]["ms"])
                ins = o["fn"](eng)
                if o["dma_key"] is not None:
                    ins.then_inc(dpool[dma_slot[o["dma_key"]]], 16)
                elif o["ms"] is not None:
                    ins.then_inc(esem[ename], 1)
            if ename == "sp":
                for k, sl in dma_slot.items():
                    eng.wait_ge(dpool[sl], 16 * (dbase[sl] + dma_count[k]))

        with nc.Block() as block:
            @block.tensor
            def _(eng):
                run("pe", eng)

            @block.scalar
            def _(eng):
                run("act", eng)

            @block.vector
            def _(eng):
                run("dve", eng)

            @block.gpsimd
            def _(eng):
                run("pool", eng)

            @block.sync
            def _(eng):
                run("sp", eng)

        for k, sl in dma_slot.items():
            dbase[sl] += dma_count[k]
        self.msbase = mcount


def build_program(debug=False):
    nc = bass.Bass("TRN2", target_bir_lowering=False)
    dt_in = lambda name, shape: nc.dram_tensor(name, shape, F32, kind="ExternalInput").ap()
    x = dt_in("x", [S, D])
    c_col = dt_in("c_col", [128, 8])
    w_ada = dt_in("w_ada", [D, 6 * D])
    b_ada = dt_in("b_ada", [1, 6 * D])
    n1g = dt_in("n1g", [1, D])
    n2g = dt_in("n2g", [1, D])
    w_in = dt_in("w_in", [D, 8192])
    qg_col = dt_in("qg_col", [128, 1])
    kg_col = dt_in("kg_col", [128, 1])
    cw_col = dt_in("cw_col", [128, 4, LRU_N])
    cb_col = dt_in("cb_col", [128, LRU_N])
    brg_col = dt_in("brg_col", [128, LRU_N])
    big_col = dt_in("big_col", [128, LRU_N])
    lam_col = dt_in("lam_col", [128, LRU_N])
    w_rg = dt_in("w_rg", [LRU_N, 128, 128])
    w_ig = dt_in("w_ig", [LRU_N, 128, 128])
    wpa_d = dt_in("wpa", [D, D])
    wpl_d = dt_in("wpl", [1536, D])
    wo_d = dt_in("wo", [D, D])
    wf1_d = dt_in("wf1", [D, 5632])
    wf2_d = dt_in("wf2", [2816, D])
    out = nc.dram_tensor("out", [S, D], F32, kind="ExternalOutput").ap()
    skind = "ExternalOutput" if debug else "Internal"
    oT_d = nc.dram_tensor("oT_d", [8, 128, S], BF16, kind=skind).ap()
    yT_d = nc.dram_tensor("yT_d", [LRU_N, 128, S], BF16, kind=skind).ap()
    x1_d = nc.dram_tensor("x1_d", [S, D], F32, kind=skind).ap()
    aT_d = nc.dram_tensor("aT_d", [FFN_N, 128, S], BF16, kind=skind).ap()
    mA_d = nc.dram_tensor("mA_d", [8, 128, S], F32, kind=skind).ap()
    gl_d = nc.dram_tensor("gl_d", [LRU_N, 128, S], F32, kind=skind).ap()
    mods_d = nc.dram_tensor("mods_d", [128, 4, D], F32, kind=skind).ap()
    mT_d = nc.dram_tensor("mT_d", [8, 128, S], BF16, kind=skind).ap()

    wview = lambda w, c0, cn: w[:, c0:c0 + cn].rearrange("(k p) c -> p k c", p=128)

    with contextlib.ExitStack() as gst:
        GT = lambda name, shape, dt: gst.enter_context(nc.sbuf_tensor(name, shape, dt))
        P = Prog(nc, gst)
        PP = [gst.enter_context(nc.psum_tensor("PP%d" % i, [128, 1024], F32)) for i in range(4)]
        pb = []
        for i in range(4):
            pb += [PP[i][:, 0:512], PP[i][:, 512:1024]]
        pbt = pb[7].bitcast(BF16).rearrange("p (k t) -> p k t", k=8)
        hT = GT("hT", [128, 8, S], BF16)
        ident = GT("ident", [128, 128], BF16)
        ones_b = GT("ones_b", [128, 128], BF16)
        tri = GT("tri", [128, 128], BF16)
        ub = GT("ub", [128, 128], BF16)
        maskb = GT("maskb", [128, 4, 512], BF16)

        def dma(q, out_, in_, key, reads=(), writes=()):
            P.op(q, lambda e: e.dma_start(out=out_, in_=in_), reads=reads, writes=writes, dma_key=key)

        def mm_group(bank, pairs, reads, wkey):
            n = len(pairs)
            for i, (l, r) in enumerate(pairs):
                P.op("pe", lambda e, l=l, r=r, i=i: e.matmul(bank, lhsT=l, rhs=r, start=(i == 0),
                                                            stop=(i == n - 1)),
                     reads=reads, writes=[wkey])

        def norm_front(i, xt_ap, xt_key, gs_ap, sh_ap, bufs):
            junk, ss, sq, rstd, tmp, hb = bufs
            j = i % 2
            P.op("act", lambda e: e.activation(out=junk[:], in_=xt_ap, func=AF.Square, accum_out=ss[j][:]),
                 reads=[xt_key], writes=["junk", ("ss", j)])
            P.op("act", lambda e: e.activation(out=sq[j][:], in_=ss[j][:], func=AF.Sqrt, scale=1.0 / D, bias=EPS),
                 reads=[("ss", j)], writes=[("sq", j)])
            P.op("dve", lambda e: e.reciprocal(out=rstd[j][:], in_=sq[j][:]), reads=[("sq", j)], writes=[("rstd", j)])
            P.op("dve", lambda e: e.scalar_tensor_tensor(out=tmp[j][:], in0=xt_ap, scalar=rstd[j][:], in1=gs_ap,
                                                         op0=ALU.mult, op1=ALU.mult),
                 reads=[xt_key, ("rstd", j), "modsA"], writes=[("ntmp", j)])
            P.op("pool", lambda e: e.tensor_tensor(out=hb[j][:], in0=tmp[j][:], in1=sh_ap, op=ALU.add),
                 reads=[("ntmp", j), "modsA"], writes=[("hb", j)])

        def norm_back(i, bufs):
            hb = bufs[5]
            j = i % 2
            for k in range(8):
                P.op("pe", lambda e, k=k: e.transpose(out=pbt[:, k, :], in_=hb[j][:, k * 128:(k + 1) * 128],
                                                      identity=ident[:]),
                     reads=[("hb", j), "consts"], writes=["pbt"])
            P.op("act", lambda e: e.copy(out=hT[:, :, i * 128:(i + 1) * 128], in_=pbt[:]),
                 reads=["pbt"], writes=[("hT", i // 4)])

        with contextlib.ExitStack() as st:
            T = lambda name, shape, dt: st.enter_context(nc.sbuf_tensor(name, shape, dt))
            cf = T("cf", [128, 128], F32)
            mf = T("mf", [128, 4, 512], F32)
            c_sb = T("c_sb", [128, 8], F32)
            c_act = T("c_act", [128, 8], F32)
            c_rep = T("c_rep", [128, 8, 128], BF16)
            wa = [T("wa%d" % i, [128, 8, 512], BF16) for i in range(2)]
            bb = [T("bb%d" % i, [128, 512], F32) for i in range(2)]
            g_bc = [T("g_bc%d" % i, [128, D], F32) for i in range(2)]
            modsA = T("modsA", [128, 2, D], F32)
            mods = T("mods", [128, 4, D], F32)
            gate1, sh2, gs2, gate2 = (mods[:, j, :] for j in range(4))
            sh1, gs1 = modsA[:, 0, :], modsA[:, 1, :]
            xt = [T("xt%d" % i, [128, D], F32) for i in range(3)]
            nbufs = (T("junk", [128, D], BF16),
                     [T("ss%d" % i, [128, 1], F32) for i in range(2)],
                     [T("sq%d" % i, [128, 1], F32) for i in range(2)],
                     [T("rstd%d" % i, [128, 1], F32) for i in range(2)],
                     [T("ntmp%d" % i, [128, D], F32) for i in range(2)],
                     [T("hb%d" % i, [128, D], BF16) for i in range(2)])

            def const_mat(dst, pattern, cmp, base, cm):
                P.op("pool", lambda e: e.memset(cf[:], 1.0), writes=["cf"])
                if pattern is not None:
                    P.op("pool", lambda e: e.affine_select(out=cf[:], in_=cf[:], pattern=pattern, compare_op=cmp,
                                                           fill=0.0, base=base, channel_multiplier=cm),
                         reads=["cf"], writes=["cf"])
                P.op("dve", lambda e: e.tensor_copy(out=dst[:], in_=cf[:]), reads=["cf"], writes=["consts"])

            const_mat(ident, [[-1, 128]], ALU.is_equal, 0, 1)
            const_mat(tri, [[-1, 128]], ALU.is_ge, 0, 1)
            const_mat(ub, [[1, 128]], ALU.is_gt, 0, -1)
            const_mat(ones_b, None, None, 0, 0)
            P.op("pool", lambda e: e.memset(mf[:], 1.0), writes=["mf"])
            for i in range(4):
                P.op("pool", lambda e, i=i: e.affine_select(out=mf[:, i, :], in_=mf[:, i, :], pattern=[[1, 512]],
                                                            compare_op=ALU.is_gt, fill=0.0, base=-128 * i,
                                                            channel_multiplier=-1), reads=["mf"], writes=["mf"])
            P.op("dve", lambda e: e.tensor_copy(out=maskb[:], in_=mf[:]), reads=["mf"], writes=["consts"])
            P.op("pool", lambda e: e.memset(cf[:], 1.0), reads=["consts"], writes=["cf"])

            dma("sp", c_sb[:], c_col, "c", writes=["c_sb"])
            dma("sp", g_bc[0][:], n1g[0, :].partition_broadcast(128), "g0", writes=["g_bc0"])
            dma("sp", g_bc[1][:], n2g[0, :].partition_broadcast(128), "g1", writes=["g_bc1"])
            P.op("act", lambda e: e.activation(out=c_act[:], in_=c_sb[:], func=AF.Silu), reads=["c_sb"], writes=["c_act"])
            for k in range(8):
                P.op("dve", lambda e, k=k: e.tensor_scalar(out=c_rep[:, k, :], in0=cf[:], scalar1=c_act[:, k:k + 1],
                                                           scalar2=None, op0=ALU.mult),
                     reads=["cf", "c_act"], writes=["c_rep"])
            mod_dst = [sh1, gs1, gate1, sh2, gs2, gate2]
            for g in range(12):
                j = g % 2
                m, half = divmod(g, 2)
                dma("pool", wa[j][:], wview(w_ada, g * 512, 512), ("wa", j), writes=[("wa", j)])
                dma("sp", bb[j][:], b_ada[0, g * 512:(g + 1) * 512].partition_broadcast(128), ("bb", j),
                    writes=[("bb", j)])
                mm_group(pb[j][:], [(c_rep[:, k, :], wa[j][:, k, :]) for k in range(8)],
                         reads=["c_rep", ("wa", j)], wkey=("pb", j))
                dst = mod_dst[m][:, half * 512:(half + 1) * 512]
                mkey = "modsA" if m < 2 else "mods"
                P.op("dve", lambda e, dst=dst, j=j: e.tensor_tensor(out=dst, in0=pb[j][:], in1=bb[j][:], op=ALU.add),
                     reads=[("pb", j), ("bb", j)], writes=[mkey])
                if m in (1, 4):
                    gsrc = g_bc[0 if m == 1 else 1][:, half * 512:(half + 1) * 512]
                    P.op("dve", lambda e, dst=dst, gsrc=gsrc: e.scalar_tensor_tensor(
                        out=dst, in0=dst, scalar=1.0, in1=gsrc, op0=ALU.add, op1=ALU.mult),
                         reads=[mkey, "g_bc0", "g_bc1"], writes=[mkey])
            dma("sp", mods_d, mods[:], "modst", reads=["mods"])
            for i in range(NTILE):
                j = i % 3
                dma("sp", xt[j][:], x[i * 128:(i + 1) * 128, :], ("xt", j), writes=[("xt", j)])
                norm_front(i, xt[j][:], ("xt", j), gs1, sh1, nbufs)
                if i >= 1:
                    norm_back(i - 1, nbufs)
            norm_back(NTILE - 1, nbufs)
            P.emit("A")

        with contextlib.ExitStack() as wst:
            wga = wst.enter_context(nc.sbuf_tensor("wga", [128, 8, D], BF16))
            wpa = wst.enter_context(nc.sbuf_tensor("wpa_s", [128, 8, D], BF16))
            with contextlib.ExitStack() as st:
                T = lambda name, shape, dt: st.enter_context(nc.sbuf_tensor(name, shape, dt))
                wq = [T("wq%d" % i, [128, 8, 128], BF16) for i in range(2)]
                wk = [T("wk%d" % i, [128, 8, 128], BF16) for i in range(2)]
                wv = T("wv", [128, 8, 256], BF16)
                qT = [T("qT%d" % i, [128, S], BF16) for i in range(2)]
                kT = [T("kT%d" % i, [128, S], BF16) for i in range(2)]
                v_sb = T("v_sb", [128, NTILE, 256], BF16)
                q2 = [T("q2_%d" % i, [128, 512], BF16) for i in range(2)]
                sd = [T("sd%d" % i, [128, 512], F32) for i in range(2)]
                rs = [T("rs%d" % i, [128, 512], F32) for i in range(2)]
                e_b = [T("e_b%d" % i, [128, 1024], F32) for i in range(3)]
                L_b = [T("L_b%d" % i, [128, 1024], BF16) for i in range(2)]
                g_b = [T("g_b%d" % i, [128, 1024], F32) for i in range(2)]
                w_b = [T("w_b%d" % i, [128, 1024], BF16) for i in range(2)]
                o_sb = T("o_sb", [128, 1024], BF16)
                gq_raw = T("gq_raw", [128, 1], F32)
                gq = T("gq", [128, 1], F32)
                gk = T("gk", [128, 1], F32)
                z2 = [PP[0], PP[1]]
                C2, O2 = PP[2], PP[3]
                zb = [PP[0][:, 0:512], PP[1][:, 0:512]]
                sb_ = [PP[2][:, 0:512], PP[3][:, 0:512]]
                sbk = ["C", "O"]

                dma("sp", gq_raw[:], qg_col, "gq", writes=["gq_raw"])
                dma("sp", gk[:], kg_col, "gk", writes=["gk"])
                P.op("act", lambda e: e.mul(gq[:], gq_raw[:], 128.0 ** -0.5), reads=["gq_raw"], writes=["gq"])

                pcount = [0]

                def qk_proj(W, wkey, gcol, gkey, dst, dkey):
                    for tc in range(NCH):
                        t = pcount[0] % 2
                        pcount[0] += 1
                        sl = slice(tc * 512, (tc + 1) * 512)
                        mm_group(zb[t][:], [(W[:, k, :], hT[:, k, sl]) for k in range(8)],
                                 reads=[wkey, ("hT", tc)], wkey=("z", t))
                        P.op("act", lambda e, t=t: e.activation(out=q2[t][:], in_=zb[t][:], func=AF.Square),
                             reads=[("z", t)], writes=[("q2", t)])
                        P.op("pe", lambda e, t=t: e.matmul(sb_[t][:], lhsT=ones_b[:], rhs=q2[t][:], start=True, stop=True),
                             reads=[("q2", t)], writes=[sbk[t]])
                        P.op("act", lambda e, t=t: e.activation(out=sd[t][:], in_=sb_[t][:], func=AF.Ln, scale=1.0 / 128,
                                                                bias=EPS), reads=[sbk[t]], writes=[("sd", t)])
                        P.op("act", lambda e, t=t: e.activation(out=rs[t][:], in_=sd[t][:], func=AF.Exp, scale=-0.5),
                             reads=[("sd", t)], writes=[("rs", t)])
                        P.op("dve", lambda e, t=t, sl=sl: e.scalar_tensor_tensor(
                            out=dst[:, sl], in0=zb[t][:], scalar=gcol[:], in1=rs[t][:], op0=ALU.mult, op1=ALU.mult),
                             reads=[("z", t), ("rs", t), gkey], writes=[(dkey, tc)])

                for p in range(4):
                    for s in range(2):
                        h = 2 * p + s
                        dma("pool", wq[s][:], wview(w_in, h * 128, 128), ("wq", s), writes=[("wq", s)])
                        dma("pool", wk[s][:], wview(w_in, 1024 + h * 128, 128), ("wk", s), writes=[("wk", s)])
                    dma("pool", wv[:], wview(w_in, 2048 + p * 256, 256), "wv", writes=["wv"])
                    if p == 1:
                        dma("pool", wga[:], wview(w_in, 6144, D), "wga")
                        dma("pool", wpa[:], wpa_d.rearrange("(k p) c -> p k c", p=128), "wpa")
                    for s in range(2):
                        qk_proj(wq[s], ("wq", s), gq, "gq", qT[s], ("qT", s))
                        qk_proj(wk[s], ("wk", s), gk, "gk", kT[s], ("kT", s))
                    for i in range(NTILE):
                        t = i % 2
                        mm_group(sb_[t][:, 0:256], [(hT[:, k, i * 128:(i + 1) * 128], wv[:, k, :]) for k in range(8)],
                                 reads=["wv", ("hT", i // 4)], wkey=sbk[t])
                        if t == 0:
                            P.op("dve", lambda e, i=i, t=t: e.tensor_copy(out=v_sb[:, i, :], in_=sb_[t][:, 0:256]),
                                 reads=[sbk[t]], writes=[("v", i)])
                        else:
                            P.op("act", lambda e, i=i, t=t: e.copy(out=v_sb[:, i, :], in_=sb_[t][:, 0:256]),
                                 reads=[sbk[t]], writes=[("v", i)])

                    steps = [(c, kb) for c in range(NCH) for kb in range(4 * c + 3, -1, -1)]
                    n = len(steps)
                    H = [slice(0, 512), slice(512, 1024)]

                    def QK(r):
                        c, kb = steps[r]
                        for s in range(2):
                            P.op("pe", lambda e, s=s: e.matmul(z2[r % 2][:, H[s]], lhsT=kT[s][:, kb * 128:(kb + 1) * 128],
                                                               rhs=qT[s][:, c * 512:(c + 1) * 512], start=True, stop=True),
                                 reads=[(("kT", s), kb // 4), (("qT", s), c)], writes=[("z", r % 2)])

                    def Bq(r):
                        c, kb = steps[r]
                        P.op("act", lambda e: e.activation(out=e_b[r % 3][:], in_=z2[r % 2][:], func=AF.Exp),
                             reads=[("z", r % 2)], writes=[("e", r % 3)])
                        if kb >= 4 * c:
                            i = kb - 4 * c
                            for s in range(2):
                                P.op("dve", lambda e, s=s: e.tensor_tensor(out=e_b[r % 3][:, H[s]], in0=e_b[r % 3][:, H[s]],
                                                                           in1=maskb[:, i, :], op=ALU.mult),
                                     reads=[("e", r % 3)], writes=[("e", r % 3)])

                    def Cq(r):
                        c, kb = steps[r]
                        P.op("act", lambda e: e.activation(out=L_b[r % 2][:], in_=e_b[r % 3][:], func=AF.Ln, bias=1.0),
                             reads=[("e", r % 3)], writes=[("L", r % 2)])
                        for s in range(2):
                            P.op("pe", lambda e, s=s: e.matmul(C2[:, H[s]], lhsT=tri[:], rhs=L_b[r % 2][:, H[s]],
                                                               start=(kb == 4 * c + 3), stop=True,
                                                               skip_group_check=(kb != 4 * c + 3)),
                                 reads=[("L", r % 2)], writes=["C"])

                    def Eq(r):
                        P.op("act", lambda e: e.activation(out=g_b[r % 2][:], in_=C2[:], func=AF.Exp, scale=-1.0),
                             reads=["C"], writes=[("g", r % 2)])

                    def Tail(r):
                        c, kb = steps[r]
                        if kb != 0:
                            for s in range(2):
                                P.op("pe", lambda e, s=s: e.matmul(C2[:, H[s]], lhsT=ub[:], rhs=L_b[r % 2][:, H[s]],
                                                                   start=False, stop=True, skip_group_check=True),
                                     reads=[("L", r % 2)], writes=["C"])
                        P.op("dve", lambda e: e.tensor_tensor(out=w_b[r % 2][:], in0=e_b[r % 3][:], in1=g_b[r % 2][:],
                                                              op=ALU.mult),
                             reads=[("e", r % 3), ("g", r % 2)], writes=[("w", r % 2)])
                        for s in range(2):
                            P.op("pe", lambda e, s=s: e.matmul(O2[:, H[s]], lhsT=v_sb[:, kb, s * 128:(s + 1) * 128],
                                                               rhs=w_b[r % 2][:, H[s]],
                                                               start=(kb == 4 * c + 3), stop=(kb == 0)),
                                 reads=[("w", r % 2), ("v", kb)], writes=["O"])
                        if kb == 0:
                            P.op("dve", lambda e: e.tensor_copy(out=o_sb[:], in_=O2[:]), reads=["O"], writes=["o_sb"])
                            for s in range(2):
                                dma("sp", oT_d[2 * p + s, :, c * 512:(c + 1) * 512], o_sb[:, H[s]], ("ost", s),
                                    reads=["o_sb"])

                    QK(0)
                    for r in range(n + 1):
                        if r + 1 < n:
                            QK(r + 1)
                        if r < n:
                            Bq(r)
                        if r >= 1:
                            Eq(r - 1)
                        if r < n:
                            Cq(r)
                        if r >= 1:
                            Tail(r - 1)
                P.emit("B")

            with contextlib.ExitStack() as st:
                T = lambda name, shape, dt: st.enter_context(nc.sbuf_tensor(name, shape, dt))
                oTc = [T("oTc%d" % i, [128, 8, 512], BF16) for i in range(2)]
                sga = [T("sga%d" % i, [128, 512], F32) for i in range(2)]
                m1 = [T("m1_%d" % i, [128, 512], F32) for i in range(2)]
                wgrA = T("wgrA", [128, 8, 1536], BF16)
                gst = [T("gst%d" % i, [128, 512], F32) for i in range(4)]
                gcnt = [0]
                dma("pool", wgrA[:], wview(w_in, 4608, 1536), "wgrA", writes=["wgrA"])
                def load_oT(c):
                    dma("sp", oTc[c % 2][:], oT_d[:, :, c * 512:(c + 1) * 512].rearrange("h p t -> p h t"),
                        ("oTc", c % 2), writes=[("oTc", c % 2)])

                load_oT(0)
                for c in range(NCH):
                    cj = c % 2
                    sl = slice(c * 512, (c + 1) * 512)
                    if c + 1 < NCH:
                        load_oT(c + 1)
                    for d in range(8):
                        dj = d % 2
                        dsl = slice(d * 128, (d + 1) * 128)
                        mm_group(pb[dj][:], [(wga[:, k, dsl], hT[:, k, sl]) for k in range(8)],
                                 reads=[("hT", c)], wkey=("pb", dj))
                        P.op("act", lambda e, dj=dj: e.activation(out=sga[dj][:], in_=pb[dj][:], func=AF.Sigmoid),
                             reads=[("pb", dj)], writes=[("sga", dj)])
                        mm_group(pb[2 + dj][:], [(wpa[:, a, dsl], oTc[cj][:, a, :]) for a in range(8)],
                                 reads=[("oTc", cj)], wkey=("pb", 2 + dj))
                        P.op("dve", lambda e, dj=dj: e.tensor_tensor(out=m1[dj][:], in0=pb[2 + dj][:], in1=sga[dj][:],
                                                                    op=ALU.mult),
                             reads=[("pb", 2 + dj), ("sga", dj)], writes=[("m1", dj)])
                        dma("sp", mA_d[d, :, sl], m1[dj][:], ("m1st", dj), reads=[("m1", dj)])
                    for n_ in range(LRU_N):
                        t = 4 + gcnt[0] % 3
                        gj = gcnt[0] % 4
                        gcnt[0] += 1
                        mm_group(pb[t][:], [(wgrA[:, k, n_ * 128:(n_ + 1) * 128], hT[:, k, sl]) for k in range(8)],
                                 reads=["wgrA", ("hT", c)], wkey=("pb", t))
                        P.op("act", lambda e, t=t, gj=gj: e.activation(out=gst[gj][:], in_=pb[t][:],
                                                                       func=AF.Gelu_apprx_tanh),
                             reads=[("pb", t)], writes=[("gst", gj)])
                        dma("sp", gl_d[n_, :, sl], gst[gj][:], ("gstst", gj), reads=[("gst", gj)])
                P.emit("D1")

        with contextlib.ExitStack() as wst:
            wgb = wst.enter_context(nc.sbuf_tensor("wgb", [128, 8, D], BF16))
            wpl = wst.enter_context(nc.sbuf_tensor("wpl_s", [128, LRU_N, D], BF16))
            with contextlib.ExitStack() as st:
                T = lambda name, shape, dt: st.enter_context(nc.sbuf_tensor(name, shape, dt))
                TP = 1024
                NQ = S // TP
                X1 = [T("X1_%d" % i, [128, TP + 4], F32) for i in range(3)]
                X2 = [T("X2_%d" % i, [128, TP], F32) for i in range(3)]
                X3 = [T("X3_%d" % i, [128, TP], F32) for i in range(2)]
                X4 = [T("X4_%d" % i, [128, TP], F32) for i in range(2)]
                X5 = [T("X5_%d" % i, [128, TP], F32) for i in range(2)]
                GL = [T("GL_%d" % i, [128, TP], F32) for i in range(3)]
                yb = [T("yb%d" % i, [128, TP], BF16) for i in range(2)]
                hc = T("hc", [128, 2], F32)
                wxr = [T("wxr%d" % i, [128, 8, 128], BF16) for i in range(3)]
                wrg = [T("wrg%d" % i, [128, 128], F32) for i in range(3)]
                wig = [T("wig%d" % i, [128, 128], F32) for i in range(3)]
                cw = T("cw", [128, 4, LRU_N], F32)
                cb = T("cb", [128, LRU_N], F32)
                brg = T("brg", [128, LRU_N], F32)
                big = T("big", [128, LRU_N], F32)
                lam = T("lam", [128, LRU_N], F32)
                cA = T("cA", [128, LRU_N], F32)
                cA2 = T("cA2", [128, LRU_N], F32)
                dma("sp", cw[:], cw_col, "cw", writes=["cols"])
                dma("sp", cb[:], cb_col, "cb", writes=["cols"])
                dma("sp", brg[:], brg_col, "brg", writes=["brg"])
                dma("sp", big[:], big_col, "big", writes=["big"])
                dma("sp", lam[:], lam_col, "lam", writes=["lam"])
                P.op("act", lambda e: e.mul(brg[:], brg[:], -1.0), reads=["brg"], writes=["brg"])
                P.op("act", lambda e: e.mul(big[:], big[:], -1.0), reads=["big"], writes=["big"])
                P.op("act", lambda e: e.activation(out=cA[:], in_=lam[:], func=AF.Exp, scale=-1.0), reads=["lam"], writes=["cA"])
                P.op("act", lambda e: e.activation(out=cA2[:], in_=cA[:], func=AF.Ln, bias=1.0), reads=["cA"], writes=["cA2"])
                P.op("act", lambda e: e.mul(cA[:], cA2[:], -8.0), reads=["cA2"], writes=["cA"])
                P.op("act", lambda e: e.mul(cA2[:], cA2[:], -16.0), reads=["cA2", "cA"], writes=["cA2"])
                units = [(n_, q) for n_ in range(LRU_N) for q in range(NQ)]
                bank = [0]

                def nb():
                    t = bank[0] % 7
                    bank[0] += 1
                    return t

                def load_w(n_):
                    j = n_ % 3
                    dma("pool", wxr[j][:], wview(w_in, 3072 + n_ * 128, 128), ("wxr", j), writes=[("wxr", j)])
                    dma("sp", wrg[j][:], w_rg[n_], ("wrg", j), writes=[("wrg", j)])
                    dma("sp", wig[j][:], w_ig[n_], ("wig", j), writes=[("wig", j)])

                def S1(u):
                    n_, q = units[u]
                    b, j = u % 3, n_ % 3
                    tok0 = q * TP
                    if q == 0:
                        if n_ + 1 < LRU_N:
                            load_w(n_ + 1)
                        P.op("pool", lambda e: e.memset(X1[b][:, 0:4], 0.0), writes=[("X1h", b)])
                    if u == 3:
                        dma("pool", wgb[:], wview(w_in, 7168, D), "wgb")
                    if u == 5:
                        dma("pool", wpl[:], wpl_d.rearrange("(k p) c -> p k c", p=128), "wpl")
                    dma("sp", GL[b][:], gl_d[n_, :, tok0:tok0 + TP], ("GL", b), writes=[("GL", b)])
                    for half in range(2):
                        t = nb()
                        tok = tok0 + half * 512
                        mm_group(pb[t][:], [(wxr[j][:, k, :], hT[:, k, tok:tok + 512]) for k in range(8)],
                                 reads=[("wxr", j), ("hT", tok // 512)], wkey=("pb", t))
                        P.op("act", lambda e, t=t, half=half: e.copy(out=X1[b][:, 4 + half * 512:4 + (half + 1) * 512],
                                                                     in_=pb[t][:]),
                             reads=[("pb", t)], writes=[("X1", b)])
                    if q < NQ - 1:
                        P.op("pool", lambda e: e.tensor_copy(out=X1[(u + 1) % 3][:, 1:4], in_=X1[b][:, TP + 1:TP + 4]),
                             reads=[("X1", b)], writes=[("X1h", (u + 1) % 3)])
                    P.op("dve", lambda e: e.tensor_scalar(out=X2[b][:], in0=X1[b][:, 4:4 + TP], scalar1=cw[:, 3, n_:n_ + 1],
                                                          scalar2=cb[:, n_:n_ + 1], op0=ALU.mult, op1=ALU.add),
                         reads=[("X1", b), "cols"], writes=[("X2", b)])
                    for kk in range(3):
                        P.op("dve", lambda e, kk=kk: e.scalar_tensor_tensor(
                            out=X2[b][:], in0=X1[b][:, 1 + kk:1 + kk + TP], scalar=cw[:, kk, n_:n_ + 1], in1=X2[b][:],
                            op0=ALU.mult, op1=ALU.add), reads=[("X1", b), ("X1h", b), ("X2", b), "cols"],
                             writes=[("X2", b)])

                def S2(u):
                    n_, q = units[u]
                    b, j, a3 = u % 2, n_ % 3, u % 3
                    tok0 = q * TP
                    g = a3
                    for (wg, wgk, nbcol, nbk, dstB, dk) in ((wrg, "wrg", brg, "brg", X3, "X3"),
                                                          (wig, "wig", big, "big", X5, "X5")):
                        for half in range(2):
                            t = nb()
                            sl = slice(half * 512, (half + 1) * 512)
                            P.op("pe", lambda e, t=t, sl=sl, wg=wg: e.matmul(pb[t][:], lhsT=wg[j][:], rhs=X2[a3][:, sl],
                                                                            start=True, stop=True),
                                 reads=[(wgk, j), ("X2", a3)], writes=[("pb", t)])
                            P.op("act", lambda e, t=t, sl=sl, nbcol=nbcol: e.activation(
                                out=X4[b][:, sl], in_=pb[t][:], func=AF.Exp, scale=-1.0, bias=nbcol[:, n_:n_ + 1]),
                                 reads=[("pb", t), nbk], writes=[("X4", b)])
                        P.op("act", lambda e: e.activation(out=X4[b][:], in_=X4[b][:], func=AF.Ln, bias=1.0),
                             reads=[("X4", b)], writes=[("X4", b)])
                        P.op("act", lambda e, dstB=dstB: e.activation(out=dstB[b][:], in_=X4[b][:], func=AF.Exp,
                                                                      scale=-1.0),
                             reads=[("X4", b)], writes=[(dk, b)])
                    P.op("pool", lambda e: e.tensor_tensor(out=X5[b][:], in0=X5[b][:], in1=X2[a3][:], op=ALU.mult),
                         reads=[("X5", b), ("X2", a3)], writes=[("X5", b)])
                    P.op("act", lambda e: e.activation(out=X4[b][:], in_=X3[b][:], func=AF.Exp, scale=cA[:, n_:n_ + 1]),
                         reads=[("X3", b), "cA"], writes=[("X4", b)])
                    P.op("act", lambda e: e.activation(out=X3[b][:], in_=X3[b][:], func=AF.Exp, scale=cA2[:, n_:n_ + 1]),
                         reads=[("X3", b), "cA2"], writes=[("X3", b)])
                    P.op("act", lambda e: e.activation(out=X3[b][:], in_=X3[b][:], func=AF.Ln, scale=-1.0, bias=1.0),
                         reads=[("X3", b)], writes=[("X3", b)])
                    P.op("act", lambda e: e.activation(out=X3[b][:], in_=X3[b][:], func=AF.Exp, scale=0.5),
                         reads=[("X3", b)], writes=[("X3", b)])
                    P.op("dve", lambda e: e.tensor_tensor(out=X5[b][:], in0=X5[b][:], in1=X3[b][:], op=ALU.mult),
                         reads=[("X5", b), ("X3", b)], writes=[("X5", b)])
                    init = 0.0 if q == 0 else hc[:, (u - 1) % 2:(u - 1) % 2 + 1]
                    P.op("dve", lambda e: e.tensor_tensor_scan(out=X3[b][:], data0=X4[b][:], data1=X5[b][:], initial=init,
                                                               op0=ALU.mult, op1=ALU.add),
                         reads=[("X4", b), ("X5", b), ("X3", b), ("hc", (u - 1) % 2)], writes=[("X3", b)])
                    P.op("dve", lambda e: e.tensor_copy(out=hc[:, u % 2:u % 2 + 1], in_=X3[b][:, TP - 1:TP]),
                         reads=[("X3", b)], writes=[("hc", u % 2)])
                    P.op("pool", lambda e: e.tensor_tensor(out=yb[b][:], in0=GL[g][:], in1=X3[b][:], op=ALU.mult),
                         reads=[("GL", g), ("X3", b)], writes=[("yb", b)])
                    dma("sp", yT_d[n_, :, tok0:tok0 + TP], yb[b][:], ("yst", b), reads=[("yb", b)])

                load_w(0)
                S1(0)
                S1(1)
                for u in range(len(units)):
                    if u + 2 < len(units):
                        S1(u + 2)
                    S2(u)
                P.emit("C")

            with contextlib.ExitStack() as wst2:
                wo = wst2.enter_context(nc.sbuf_tensor("wo_s", [128, 8, D], BF16))
                with contextlib.ExitStack() as st:
                    T = lambda name, shape, dt: st.enter_context(nc.sbuf_tensor(name, shape, dt))
                    yTc = [T("yTc%d" % i, [128, LRU_N, 512], BF16) for i in range(2)]
                    mAd = [T("mAd%d" % i, [128, 512], F32) for i in range(3)]
                    sgb = [T("sgb%d" % i, [128, 512], F32) for i in range(2)]
                    m2 = [T("m2_%d" % i, [128, 512], F32) for i in range(2)]
                    mo = [T("mo%d" % i, [128, 512], BF16) for i in range(2)]
                    dma("pool", wo[:], wo_d.rearrange("(k p) c -> p k c", p=128), "wo")
                    def load_yT(c):
                        dma("sp", yTc[c % 2][:], yT_d[:, :, c * 512:(c + 1) * 512].rearrange("h p t -> p h t"),
                            ("yTc", c % 2), writes=[("yTc", c % 2)])

                    def load_mA(it):
                        c_, d_ = divmod(it, 8)
                        dma("sp", mAd[it % 3][:], mA_d[d_, :, c_ * 512:(c_ + 1) * 512], ("mAd", it % 3),
                            writes=[("mAd", it % 3)])

                    load_yT(0)
                    load_mA(0)
                    load_mA(1)
                    for c in range(NCH):
                        cj = c % 2
                        sl = slice(c * 512, (c + 1) * 512)
                        if c + 1 < NCH:
                            load_yT(c + 1)
                        for d in range(8):
                            dj = d % 2
                            dsl = slice(d * 128, (d + 1) * 128)
                            it = c * 8 + d
                            if it + 2 < NCH * 8:
                                load_mA(it + 2)
                            mm_group(pb[dj][:], [(wgb[:, k, dsl], hT[:, k, sl]) for k in range(8)],
                                     reads=[("hT", c)], wkey=("pb", dj))
                            P.op("act", lambda e, dj=dj: e.activation(out=sgb[dj][:], in_=pb[dj][:], func=AF.Sigmoid),
                                 reads=[("pb", dj)], writes=[("sgb", dj)])
                            mm_group(pb[2 + dj][:], [(wpl[:, a, dsl], yTc[cj][:, a, :]) for a in range(LRU_N)],
                                     reads=[("yTc", cj)], wkey=("pb", 2 + dj))
                            P.op("dve", lambda e, dj=dj: e.tensor_tensor(out=m2[dj][:], in0=pb[2 + dj][:], in1=sgb[dj][:],
                                                                        op=ALU.mult),
                                 reads=[("pb", 2 + dj), ("sgb", dj)], writes=[("m2", dj)])
                            P.op("pool", lambda e, dj=dj, it=it: e.tensor_tensor(out=mo[dj][:], in0=m2[dj][:],
                                                                                     in1=mAd[it % 3][:], op=ALU.add),
                                 reads=[("m2", dj), ("mAd", it % 3)], writes=[("mo", dj)])
                            dma("sp", mT_d[d, :, sl], mo[dj][:], ("most", dj), reads=[("mo", dj)])
                    P.emit("D2")

                with contextlib.ExitStack() as st:
                    T = lambda name, shape, dt: st.enter_context(nc.sbuf_tensor(name, shape, dt))
                    mTc = [T("mTc%d" % i, [128, 8, 512], BF16) for i in range(2)]
                    xt = [T("xtD%d" % i, [128, D], F32) for i in range(2)]
                    x1t = [T("x1t%d" % i, [128, D], F32) for i in range(3)]
                    gtmp = [T("gtmp%d" % i, [128, 512], F32) for i in range(2)]
                    mods = T("modsD", [128, 4, D], F32)
                    gate1, sh2, gs2, gate2 = (mods[:, j, :] for j in range(4))
                    dma("sp", mods[:], mods_d, "modld", writes=["mods", "modsA"])
                    nbufs = (T("junkD", [128, D], BF16),
                             [T("ssD%d" % i, [128, 1], F32) for i in range(2)],
                             [T("sqD%d" % i, [128, 1], F32) for i in range(2)],
                             [T("rstdD%d" % i, [128, 1], F32) for i in range(2)],
                             [T("ntmpD%d" % i, [128, D], F32) for i in range(2)],
                             [T("hbD%d" % i, [128, D], BF16) for i in range(2)])
                    def load_mT(c):
                        dma("sp", mTc[c % 2][:], mT_d[:, :, c * 512:(c + 1) * 512].rearrange("h p t -> p h t"),
                            ("mTc", c % 2), writes=[("mTc", c % 2)])

                    load_mT(0)
                    for c in range(NCH):
                        cj = c % 2
                        sl = slice(c * 512, (c + 1) * 512)
                        if c + 1 < NCH:
                            load_mT(c + 1)
                        for jt in range(4):
                            i = 4 * c + jt
                            ij = i % 2
                            dma("sp", xt[ij][:], x[i * 128:(i + 1) * 128, :], ("xtD", ij), writes=[("xtD", ij)])
                            for half in range(2):
                                hs = slice(half * 512, (half + 1) * 512)
                                bi = 2 * (i % 2) + half
                                bk = pb[bi]
                                mm_group(bk[:], [(mTc[cj][:, d, jt * 128:(jt + 1) * 128], wo[:, d, hs]) for d in range(8)],
                                         reads=[("mTc", cj)], wkey=("pb", bi))
                                P.op("dve", lambda e, hs=hs, bk=bk, half=half: e.tensor_tensor(
                                    out=gtmp[half][:], in0=bk[:], in1=gate1[:, hs], op=ALU.mult),
                                     reads=[("pb", bi), "mods"], writes=[("gtmp", half)])
                                P.op("dve", lambda e, hs=hs, ij=ij, i3=i % 3, half=half: e.tensor_tensor(
                                    out=x1t[i3][:, hs], in0=gtmp[half][:], in1=xt[ij][:, hs], op=ALU.add),
                                     reads=[("gtmp", half), ("xtD", ij)], writes=[("x1t", i % 3)])
                            dma("sp", x1_d[i * 128:(i + 1) * 128, :], x1t[i % 3][:], ("x1st", i % 3),
                                reads=[("x1t", i % 3)])
                            if i >= 1:
                                norm_front(i - 1, x1t[(i - 1) % 3][:], ("x1t", (i - 1) % 3), gs2, sh2, nbufs)
                            if i >= 2:
                                norm_back(i - 2, nbufs)
                    norm_front(NTILE - 1, x1t[(NTILE - 1) % 3][:], ("x1t", (NTILE - 1) % 3), gs2, sh2, nbufs)
                    norm_back(NTILE - 2, nbufs)
                    norm_back(NTILE - 1, nbufs)
                    P.emit("D3")

        with contextlib.ExitStack() as wst:
            w2 = wst.enter_context(nc.sbuf_tensor("w2", [128, FFN_N, D], BF16))
            with contextlib.ExitStack() as st:
                T = lambda name, shape, dt: st.enter_context(nc.sbuf_tensor(name, shape, dt))
                wg_ = [T("wg%d" % i, [128, 8, 128], BF16) for i in range(2)]
                wu_ = [T("wu%d" % i, [128, 8, 128], BF16) for i in range(2)]
                sg = [T("sg%d" % i, [128, 512], F32) for i in range(2)]
                ab = [T("ab%d" % i, [128, S], BF16) for i in range(2)]
                cnt = 0
                for f in range(FFN_N):
                    j = f % 2
                    dma("pool", wg_[j][:], wview(wf1_d, f * 128, 128), ("wg", j), writes=[("wg", j)])
                    dma("pool", wu_[j][:], wview(wf1_d, 2816 + f * 128, 128), ("wu", j), writes=[("wu", j)])
                    if f in (2, 3):
                        hf = f - 2
                        dma("pool", w2[:, hf * 11:(hf + 1) * 11, :],
                            wf2_d[hf * 1408:(hf + 1) * 1408, :].rearrange("(k p) c -> p k c", p=128), ("w2", hf))
                    for tc in range(NCH):
                        t = cnt % 3
                        cnt += 1
                        sl = slice(tc * 512, (tc + 1) * 512)
                        bg, bu = pb[2 * t], pb[2 * t + 1]
                        mm_group(bg[:], [(wg_[j][:, k, :], hT[:, k, sl]) for k in range(8)],
                                 reads=[("wg", j), ("hT", tc)], wkey=("pb", 2 * t))
                        mm_group(bu[:], [(wu_[j][:, k, :], hT[:, k, sl]) for k in range(8)],
                                 reads=[("wu", j), ("hT", tc)], wkey=("pb", 2 * t + 1))
                        tj = cnt % 2
                        P.op("act", lambda e, bg=bg, tj=tj: e.activation(out=sg[tj][:], in_=bg[:], func=AF.Silu),
                             reads=[("pb", 2 * t)], writes=[("sg", tj)])
                        P.op("dve", lambda e, bu=bu, tj=tj, sl=sl, j=j: e.tensor_tensor(out=ab[j][:, sl], in0=bu[:],
                                                                                      in1=sg[tj][:], op=ALU.mult),
                             reads=[("pb", 2 * t + 1), ("sg", tj)], writes=[("ab", j)])
                    dma("sp", aT_d[f], ab[j][:], ("ast", j), reads=[("ab", j)])
                P.emit("E")

            with contextlib.ExitStack() as st:
                T = lambda name, shape, dt: st.enter_context(nc.sbuf_tensor(name, shape, dt))
                ac = [T("ac%d" % i, [128, FFN_N, 512], BF16) for i in range(2)]
                x1t = [T("x1F%d" % i, [128, D], F32) for i in range(2)]
                ot = [T("ot%d" % i, [128, D], F32) for i in range(2)]
                gtmp = [T("gtmpF%d" % i, [128, 512], F32) for i in range(2)]
                mods = T("modsF", [128, 4, D], F32)
                gate1, sh2, gs2, gate2 = (mods[:, j, :] for j in range(4))
                dma("sp", mods[:], mods_d, "modld", writes=["mods"])
                cnt = 0
                def load_ac(c):
                    dma("sp", ac[c % 2][:], aT_d[:, :, c * 512:(c + 1) * 512].rearrange("f p t -> p f t"),
                        ("ac", c % 2), writes=[("ac", c % 2)])

                load_ac(0)
                for c in range(NCH):
                    cj = c % 2
                    if c + 1 < NCH:
                        load_ac(c + 1)
                    for jt in range(4):
                        i = 4 * c + jt
                        ij = i % 2
                        dma("sp", x1t[ij][:], x1_d[i * 128:(i + 1) * 128, :], ("x1F", ij), writes=[("x1F", ij)])
                        for half in range(2):
                            t = cnt % 4
                            cnt += 1
                            hs = slice(half * 512, (half + 1) * 512)
                            mm_group(pb[t][:], [(ac[cj][:, f, jt * 128:(jt + 1) * 128], w2[:, f, hs]) for f in range(FFN_N)],
                                     reads=[("ac", cj)], wkey=("pb", t))
                            P.op("dve", lambda e, t=t, hs=hs, half=half: e.tensor_tensor(
                                out=gtmp[half][:], in0=pb[t][:], in1=gate2[:, hs], op=ALU.mult),
                                 reads=[("pb", t), "mods"], writes=[("gtmpF", half)])
                            P.op("pool", lambda e, hs=hs, ij=ij, half=half: e.tensor_tensor(
                                out=ot[ij][:, hs], in0=gtmp[half][:], in1=x1t[ij][:, hs], op=ALU.add),
                                 reads=[("gtmpF", half), ("x1F", ij)], writes=[("ot", ij)])
                        dma("sp", out[i * 128:(i + 1) * 128, :], ot[ij][:], ("ost", ij), reads=[("ot", ij)])
                P.emit("F")
    return nc


def _prep_inputs(inputs):
    f = lambda a: np.ascontiguousarray(np.asarray(a, dtype=np.float32))
    col = lambda v: f(np.asarray(v).reshape(-1, 128).T)
    shared = {
        "w_ada": f(inputs["w_ada"][0]),
        "b_ada": f(inputs["b_ada"][0].reshape(1, -1)),
        "n1g": f(inputs["norm1_g"][0].reshape(1, -1)),
        "n2g": f(inputs["norm2_g"][0].reshape(1, -1)),
        "w_in": f(inputs["w_in"][0]),
        "qg_col": f(inputs["q_norm_g"][0].reshape(128, 1)),
        "kg_col": f(inputs["k_norm_g"][0].reshape(128, 1)),
        "cw_col": f(np.asarray(inputs["conv_w"][0]).reshape(4, LRU_N, 128).transpose(2, 0, 1)),
        "cb_col": col(inputs["conv_b"][0]),
        "brg_col": col(inputs["b_rg"][0]),
        "big_col": col(inputs["b_ig"][0]),
        "lam_col": col(inputs["lru_lambda"][0]),
        "w_rg": f(inputs["w_rg"][0]),
        "w_ig": f(inputs["w_ig"][0]),
        "wpa": f(inputs["w_proj_attn"][0]),
        "wpl": f(inputs["w_proj_lru"][0]),
        "wo": f(inputs["w_out"][0]),
        "wf1": f(inputs["w_ffn_in"][0]),
        "wf2": f(inputs["w_ffn_out"][0]),
    }
    xs = np.asarray(inputs["x"], dtype=np.float32)
    cs = np.asarray(inputs["c"], dtype=np.float32)
    in_maps = []
    for b in range(8):
        m = dict(shared)
        m["x"] = np.ascontiguousarray(xs[b])
        m["c_col"] = col(cs[b])
        in_maps.append(m)
    return in_maps


def kernel(**inputs):
    nc = build_program()
    in_maps = _prep_inputs(inputs)
    res = run_bass_kernel_spmd(nc, in_maps, core_ids=list(range(8)))
    return np.stack([np.asarray(r["out"], dtype=np.float32) for r in res.results], axis=0)
```
